# Optimizing a Trainium2 kernel written in Bass

```python
import math
import jax, jax.numpy as jnp
from jax import lax
import numpy as np

D_MODEL = 1024
BATCH = 8
SEQ = 2048
DEPTH = 4

GRID_W = 64
CTX_LEN = 256
N_MOD = 9
D_FF = int(math.ceil(8 * D_MODEL / 3 / 128)) * 128
FFN_RES = 0.5
ROPE_BASE = 10000.0
EPS = 1e-6
NEG_INF = -1e30

A_WIDTH = D_MODEL // 4
A_HEADS = 4
A_HEAD_DIM = A_WIDTH // A_HEADS
CHUNK = 128
B_WIDTH = D_MODEL // 2
B_HEADS = 4
B_V_DIM = B_WIDTH // B_HEADS
B_QK_DIM = B_V_DIM // 2
C_WIDTH = D_MODEL // 4
C_HEAD_DIM = 64
C_HEADS = C_WIDTH // C_HEAD_DIM
C_KV_HEADS = 2
C_GROUP = C_HEADS // C_KV_HEADS
WINDOW = 128
BLK = 128
D_MIX = A_WIDTH + B_WIDTH + C_WIDTH

W_UV_A = 2 * A_WIDTH
W_Q_B = B_HEADS * 2 * B_QK_DIM
W_Q_C = C_HEADS * C_HEAD_DIM
W_K_B = B_HEADS * 2 * B_QK_DIM
W_V_B = B_HEADS * B_V_DIM
W_K_C = C_KV_HEADS * C_HEAD_DIM
W_V_C = C_KV_HEADS * C_HEAD_DIM
IN_COLS = W_UV_A + W_Q_B + W_Q_C + W_K_B + W_V_B + W_K_C + W_V_C
KV_START = W_UV_A + W_Q_B + W_Q_C
SPLITS = (W_UV_A, W_UV_A + W_Q_B, KV_START, KV_START + W_K_B, KV_START + W_K_B + W_V_B,
          KV_START + W_K_B + W_V_B + W_K_C)
KV_SPLITS = (W_K_B, W_K_B + W_V_B, W_K_B + W_V_B + W_K_C)

kernel_name = 'hybrid_diffusion_trunk'


def rmsnorm(x, g):
    xf = x.astype(jnp.float32)
    y = xf * lax.rsqrt(jnp.mean(xf * xf, axis=-1, keepdims=True) + EPS)
    return (y * g.astype(jnp.float32)).astype(x.dtype)


def axial_rope_tables(rows, dim):
    row = jnp.repeat(jnp.arange(rows, dtype=jnp.float32), GRID_W)
    col = jnp.tile(jnp.arange(GRID_W, dtype=jnp.float32), rows)
    quarter = dim // 4
    inv = ROPE_BASE ** (-jnp.arange(quarter, dtype=jnp.float32) / quarter)
    ar = row[:, None] * inv[None, :]
    ac = col[:, None] * inv[None, :]
    ang = jnp.concatenate([ar, ar, ac, ac], axis=-1)
    return jnp.cos(ang), jnp.sin(ang)


def apply_rope(x, cos, sin):
    shape = (x.shape[1],) + (1,) * (x.ndim - 3) + (x.shape[-1],)
    cos = cos.reshape(shape).astype(x.dtype)
    sin = sin.reshape(shape).astype(x.dtype)
    x1, x2, x3, x4 = jnp.split(x, 4, axis=-1)
    rot = jnp.concatenate([-x2, x1, -x4, x3], axis=-1)
    return x * cos + rot * sin


def adaln_in(s, g_pre, shift, scale):
    return rmsnorm(s, g_pre) * (1 + scale) + shift


def adaln_out(s, y, g_post, gate, weight):
    return s + weight * gate * rmsnorm(y, g_post)


def swiglu(y, wg, wu, wd):
    return (jax.nn.silu(y @ wg) * (y @ wu)) @ wd


def ffn_sublayer(s, mod, j, g_pre, g_post, wg, wu, wd):
    y = adaln_in(s, g_pre, mod[..., 3 * j, :, :], mod[..., 3 * j + 1, :, :])
    return adaln_out(s, swiglu(y, wg, wu, wd), g_post, mod[..., 3 * j + 2, :, :], FFN_RES)


def chunk_gmlp(uv, v_gain, w_s, b_s):
    b_, l_, _ = uv.shape
    u, v = jnp.split(jax.nn.gelu(uv), 2, axis=-1)
    v = rmsnorm(v.reshape(b_, l_, A_HEADS, A_HEAD_DIM), v_gain)
    v = v.reshape(b_, l_ // CHUNK, CHUNK, A_HEADS, A_HEAD_DIM)
    mixed = jnp.einsum('hpq,bnqhc->bnphc', w_s, v) + b_s.T[:, :, None]
    return u * mixed.reshape(b_, l_, A_WIDTH)


def diff_softmax(q, k, v, lam):
    s = jnp.einsum('bqhmd,bkhmd->bhmqk', q, k).astype(jnp.float32) * (B_QK_DIM ** -0.5)
    p = jax.nn.softmax(s, axis=-1)
    w = p[:, :, 0] - lam * p[:, :, 1]
    return jnp.einsum('bhqk,bkhe->bqhe', w.astype(v.dtype), v)


def diff_attention_latent(q, k_all, v_all, lam):
    b_, l_ = q.shape[:2]
    nb = l_ // BLK
    qb = jnp.moveaxis(q.reshape((b_, nb, BLK) + q.shape[2:]), 1, 0)
    ob = lax.map(lambda qi: diff_softmax(qi, k_all, v_all, lam), qb)
    return jnp.moveaxis(ob, 0, 1).reshape((b_, l_) + ob.shape[3:])


def diff_post(o, g_sub, lam_init):
    return (rmsnorm(o, g_sub) * (1 - lam_init)).reshape(o.shape[0], o.shape[1], -1)


def swa_latent(q, k, v, kc, vc, sink):
    b_, l_ = q.shape[:2]
    nb = l_ // BLK
    qb = q.reshape(b_, nb, BLK, C_KV_HEADS, C_GROUP, C_HEAD_DIM)
    pad = ((0, 0), (1, 1), (0, 0), (0, 0), (0, 0))
    kp = jnp.pad(k.reshape(b_, nb, BLK, C_KV_HEADS, C_HEAD_DIM), pad)
    vp = jnp.pad(v.reshape(b_, nb, BLK, C_KV_HEADS, C_HEAD_DIM), pad)
    kband = jnp.concatenate([kp[:, :-2], kp[:, 1:-1], kp[:, 2:]], axis=2)
    vband = jnp.concatenate([vp[:, :-2], vp[:, 1:-1], vp[:, 2:]], axis=2)
    scale = C_HEAD_DIM ** -0.5
    s_band = jnp.einsum('bnqkgd,bnjkd->bnkgqj', qb, kband).astype(jnp.float32) * scale
    blk = jnp.arange(nb)
    qpos = blk[:, None] * BLK + jnp.arange(BLK)[None, :]
    kpos = (blk[:, None] - 1) * BLK + jnp.arange(3 * BLK)[None, :]
    valid = ((jnp.abs(qpos[:, :, None] - kpos[:, None, :]) <= WINDOW)
             & (kpos[:, None, :] >= 0) & (kpos[:, None, :] < l_))
    s_band = jnp.where(valid[None, :, None, None], s_band, NEG_INF)
    s_ctx = jnp.einsum('bnqkgd,bckd->bnkgqc', qb, kc).astype(jnp.float32) * scale
    s_sink = jnp.broadcast_to(sink.astype(jnp.float32)[None, None, :, :, None, None],
                              s_band.shape[:-1] + (1,))
    p = jax.nn.softmax(jnp.concatenate([s_band, s_ctx, s_sink], axis=-1), axis=-1)
    p_band = p[..., :3 * BLK].astype(v.dtype)
    p_ctx = p[..., 3 * BLK:-1].astype(v.dtype)
    out = (jnp.einsum('bnkgqj,bnjkd->bnqkgd', p_band, vband)
           + jnp.einsum('bnkgqc,bckd->bnqkgd', p_ctx, vc))
    return out.reshape(b_, l_, C_WIDTH)


def swa_context(q, kc, vc, sink):
    b_, l_ = q.shape[:2]
    s = jnp.einsum('bqkgd,bckd->bkgqc', q, kc).astype(jnp.float32) * (C_HEAD_DIM ** -0.5)
    s_sink = jnp.broadcast_to(sink.astype(jnp.float32)[None, :, :, None, None], s.shape[:-1] + (1,))
    p = jax.nn.softmax(jnp.concatenate([s, s_sink], axis=-1), axis=-1)[..., :-1]
    out = jnp.einsum('bkgqc,bckd->bqkgd', p.astype(vc.dtype), vc)
    return out.reshape(b_, l_, C_WIDTH)


def setup_inputs(seed: int = 0) -> dict:
    key = jax.random.key(seed)
    ks = jax.random.split(key, 19)
    f32 = jnp.float32
    nrm = lambda k, shape, s: jax.random.normal(k, shape, f32) * s
    return {
        'x': nrm(ks[0], (BATCH, SEQ, D_MODEL), 1.0),
        'c': nrm(ks[1], (BATCH, D_MODEL), 1.0),
        'ctx': nrm(ks[2], (BATCH, CTX_LEN, D_MODEL), 1.0),
        'c_ctx': nrm(ks[3], (D_MODEL,), 1.0),
        'w_mod': nrm(ks[4], (DEPTH, D_MODEL, N_MOD * D_MODEL), 0.5 * D_MODEL ** -0.5),
        'b_mod': nrm(ks[5], (DEPTH, N_MOD * D_MODEL), 0.01),
        'norm_pre': 1.0 + nrm(ks[6], (DEPTH, 3, D_MODEL), 0.02),
        'norm_post': 1.0 + nrm(ks[7], (DEPTH, 3, D_MODEL), 0.02),
        'ffn_w_gate': nrm(ks[8], (DEPTH, 2, D_MODEL, D_FF), D_MODEL ** -0.5),
        'ffn_w_up': nrm(ks[9], (DEPTH, 2, D_MODEL, D_FF), D_MODEL ** -0.5),
        'ffn_w_down': nrm(ks[10], (DEPTH, 2, D_FF, D_MODEL), D_FF ** -0.5),
        'w_in': nrm(ks[11], (DEPTH, D_MODEL, IN_COLS), D_MODEL ** -0.5),
        'w_out': nrm(ks[12], (DEPTH, D_MIX, D_MODEL), D_MIX ** -0.5),
        'gmlp_v_gain': 1.0 + nrm(ks[13], (DEPTH, A_HEADS, A_HEAD_DIM), 0.02),
        'gmlp_w_s': nrm(ks[14], (DEPTH, A_HEADS, CHUNK, CHUNK), CHUNK ** -0.5),
        'gmlp_b_s': nrm(ks[15], (DEPTH, A_HEADS, CHUNK), 0.02),
        'diff_lambda': nrm(ks[16], (DEPTH, 4, B_QK_DIM), 0.1),
        'diff_subln': 1.0 + nrm(ks[17], (DEPTH, B_V_DIM), 0.02),
        'swa_sink': nrm(ks[18], (DEPTH, C_HEADS), 0.5),
    }


def reference(x, c, ctx, c_ctx, w_mod, b_mod, norm_pre, norm_post, ffn_w_gate, ffn_w_up, ffn_w_down,
              w_in, w_out, gmlp_v_gain, gmlp_w_s, gmlp_b_s, diff_lambda, diff_subln, swa_sink):
    b_, l_, _ = x.shape
    c_len = ctx.shape[1]
    ROWS = l_ // GRID_W
    cos_b, sin_b = axial_rope_tables(ROWS, B_QK_DIM)
    cos_c, sin_c = axial_rope_tables(ROWS, C_HEAD_DIM)
    sc = jax.nn.silu(c)
    scc = jax.nn.silu(c_ctx)
    h = ctx
    for l in range(DEPTH):
        last = l == DEPTH - 1
        mod_x = (sc @ w_mod[l] + b_mod[l]).reshape(b_, N_MOD, 1, D_MODEL)
        mod_h = (scc @ w_mod[l] + b_mod[l]).reshape(N_MOD, 1, D_MODEL)
        lam_init = 0.8 - 0.6 * math.exp(-0.3 * l)
        lam_p = diff_lambda[l].astype(jnp.float32)
        lam = (jnp.exp(jnp.sum(lam_p[0] * lam_p[1])) - jnp.exp(jnp.sum(lam_p[2] * lam_p[3])) + lam_init)
        sink = swa_sink[l].reshape(C_KV_HEADS, C_GROUP)

        f0 = (norm_pre[l, 0], norm_post[l, 0], ffn_w_gate[l, 0], ffn_w_up[l, 0], ffn_w_down[l, 0])
        x = ffn_sublayer(x, mod_x, 0, *f0)
        h = ffn_sublayer(h, mod_h, 0, *f0)

        ax = adaln_in(x, norm_pre[l, 1], mod_x[:, 3], mod_x[:, 4])
        ah = adaln_in(h, norm_pre[l, 1], mod_h[3], mod_h[4])
        uv_x, qb_x, qc_x, kb_x, vb_x, kc_x, vc_x = jnp.split(ax @ w_in[l], SPLITS, axis=-1)
        if last:
            kb_h, vb_h, kc_h, vc_h = jnp.split(ah @ w_in[l][:, KV_START:], KV_SPLITS, axis=-1)
        else:
            uv_h, qb_h, qc_h, kb_h, vb_h, kc_h, vc_h = jnp.split(ah @ w_in[l], SPLITS, axis=-1)
        kb_h = kb_h.reshape(b_, c_len, B_HEADS, 2, B_QK_DIM)
        vb_h = vb_h.reshape(b_, c_len, B_HEADS, B_V_DIM)
        kc_h = kc_h.reshape(b_, c_len, C_KV_HEADS, C_HEAD_DIM)
        vc_h = vc_h.reshape(b_, c_len, C_KV_HEADS, C_HEAD_DIM)
        qb_x = apply_rope(qb_x.reshape(b_, l_, B_HEADS, 2, B_QK_DIM), cos_b, sin_b)
        kb_x = apply_rope(kb_x.reshape(b_, l_, B_HEADS, 2, B_QK_DIM), cos_b, sin_b)
        vb_x = vb_x.reshape(b_, l_, B_HEADS, B_V_DIM)
        qc_x = apply_rope(qc_x.reshape(b_, l_, C_KV_HEADS, C_GROUP, C_HEAD_DIM), cos_c, sin_c)
        kc_x = apply_rope(kc_x.reshape(b_, l_, C_KV_HEADS, C_HEAD_DIM), cos_c, sin_c)
        vc_x = vc_x.reshape(b_, l_, C_KV_HEADS, C_HEAD_DIM)

        o_a = chunk_gmlp(uv_x, gmlp_v_gain[l], gmlp_w_s[l], gmlp_b_s[l])
        k_all = jnp.concatenate([kb_h, kb_x], axis=1)
        v_all = jnp.concatenate([vb_h, vb_x], axis=1)
        o_b = diff_post(diff_attention_latent(qb_x, k_all, v_all, lam), diff_subln[l], lam_init)
        o_c = swa_latent(qc_x, kc_x, vc_x, kc_h, vc_h, sink)
        mix_x = jnp.concatenate([o_a, o_b, o_c], axis=-1) @ w_out[l]
        x = adaln_out(x, mix_x, norm_post[l, 1], mod_x[:, 5], 1.0)
        if not last:
            qb_h = qb_h.reshape(b_, c_len, B_HEADS, 2, B_QK_DIM)
            qc_h = qc_h.reshape(b_, c_len, C_KV_HEADS, C_GROUP, C_HEAD_DIM)
            oh_a = chunk_gmlp(uv_h, gmlp_v_gain[l], gmlp_w_s[l], gmlp_b_s[l])
            oh_b = diff_post(diff_softmax(qb_h, kb_h, vb_h, lam), diff_subln[l], lam_init)
            oh_c = swa_context(qc_h, kc_h, vc_h, sink)
            mix_h = jnp.concatenate([oh_a, oh_b, oh_c], axis=-1) @ w_out[l]
            h = adaln_out(h, mix_h, norm_post[l, 1], mod_h[5], 1.0)

        f1 = (norm_pre[l, 2], norm_post[l, 2], ffn_w_gate[l, 1], ffn_w_up[l, 1], ffn_w_down[l, 1])
        x = ffn_sublayer(x, mod_x, 2, *f1)
        if not last:
            h = ffn_sublayer(h, mod_h, 2, *f1)
    return x
```

```python
import math
import numpy as np
import concourse.bass as bass
import concourse.mybir as mybir
from concourse.bass_utils import run_bass_kernel_spmd

F32 = mybir.dt.float32
BF16 = mybir.dt.bfloat16
AF = mybir.ActivationFunctionType
ALU = mybir.AluOpType
AX = mybir.AxisListType

D = 1024
DEPTH = 4
NCTX = 256
NLAT = 2048
NT = NCTX + NLAT
DFF = 2816
NJ = DFF // 128
EPS = 1e-6
GROUPS = [(0, 256, 1), (256, 512, 0), (768, 512, 0), (1280, 512, 0), (1792, 512, 0)]
GC1 = 0.7978845608028654
ARENA_BYTES = 212000


class Tile:
    __slots__ = ("ap", "space", "lo", "hi", "w", "r", "ov")

    def __init__(self, ap, space, lo, hi):
        self.ap = ap
        self.space = space
        self.lo = lo
        self.hi = hi
        self.w = None
        self.r = {}
        self.ov = None


class Builder:
    def __init__(self, nlayers):
        self.nl = nlayers
        self.nc = bass.Bass("TRN2", target_bir_lowering=False)
        self.tiles = []
        self.cnt = {}
        self.sem = {}
        self.eng = {}
        self.waited = {}
        self.dsems = []

    def reg_engine(self, name, eng, sem):
        self.eng[name] = eng
        self.sem[name] = sem
        self.cnt[name] = 0
        self.waited[name] = {}

    def mk(self, ap, space, lo, hi):
        t = Tile(ap, space, lo, hi)
        t.ov = [t]
        for o in self.tiles:
            if o.space == space and o.lo < hi and lo < o.hi:
                o.ov.append(t)
                t.ov.append(o)
        self.tiles.append(t)
        return t

    def _wait(self, en, key, semobj, val):
        w = self.waited[en]
        if w.get(key, 0) >= val:
            return
        self.eng[en].wait_ge(semobj, val)
        w[key] = val

    def _deps(self, en, R, W):
        need = {}

        def add(mark):
            if mark is None:
                return
            key = mark[0]
            if need.get(key, (None, 0))[1] < mark[2]:
                need[key] = (mark[1], mark[2])

        for t in R:
            for o in t.ov:
                add(o.w)
        for t in W:
            for o in t.ov:
                add(o.w)
                for mk_ in o.r.values():
                    add(mk_)
        for key, (semobj, val) in need.items():
            if key == "pe" and en == "pe":
                continue
            self._wait(en, key, semobj, val)

    def op(self, en, fn, R=(), W=(), inc=True):
        self._deps(en, R, W)
        ins = fn()
        c = self.cnt[en] + 1
        if inc:
            ins.then_inc(self.sem[en], 1)
            self.cnt[en] = c
        mark = (en, self.sem[en], c)
        for t in R:
            t.r[en] = mark
        for t in W:
            t.w = mark
            t.r = {}
        return ins

    def dma(self, q, out_t, in_ap, dsem, out_ap=None, R=()):
        self._deps(q, R, [out_t])
        ins = self.eng[q].dma_start(out=(out_ap if out_ap is not None else out_t.ap), in_=in_ap)
        dsem["val"] += 16
        ins.then_inc(dsem["sem"], 16)
        out_t.w = (dsem["key"], dsem["sem"], dsem["val"])
        out_t.r = {}


def build(nlayers=DEPTH, stages=("ffn0", "mix", "ffn1", "p1", "p2q", "B", "C")):
    B = Builder(nlayers)
    nc = B.nc
    dr = {}

    def din(name, shape):
        dr[name] = nc.dram_tensor(name, list(shape), F32, kind="ExternalInput").ap()
        return dr[name]

    xT_d = din("xT", [128, 8, NT])
    cT_d = din("cT", [128, 16])
    wmod_d = din("wmod", [DEPTH, 18, 128, 8 * 512])
    bmod_d = din("bmod", [DEPTH, 128, 72])
    npre_d = din("npre", [DEPTH, 128, 24])
    npost_d = din("npost", [DEPTH, 128, 24])
    wg_d = din("wg", [DEPTH, 2, 11, 128, 8 * 256])
    wu_d = din("wu", [DEPTH, 2, 11, 128, 8 * 256])
    wd_d = din("wd", [DEPTH, 2, 8, 128, NJ * 128])
    wkv_d = din("wkv", [DEPTH, 5, 128, 8 * 256])
    wq_d = din("wq", [DEPTH, 5, 128, 8 * 256])
    wout_d = din("wout", [DEPTH, 4, 128, 8 * 256])
    wsT_d = din("wsT", [DEPTH, 128, 512])
    bsbc_d = din("bsbc", [DEPTH, 128, 256])
    vgain_d = din("vgain", [DEPTH, 128, 256])
    subln_d = din("subln", [DEPTH, 128, 128])
    dlam_d = din("dlam", [DEPTH, 128, 256])
    sink_d = din("sink", [DEPTH, 128, 4])
    cbf_d = din("cbf", [128, 4 * 128])
    cos_d = din("cosT", [128, NLAT])
    sin_d = din("sinT", [128, NLAT])
    y_d = nc.dram_tensor("yT", [128, 8, NLAT], F32, kind="ExternalOutput").ap()

    import contextlib
    with contextlib.ExitStack() as es:
        arena = es.enter_context(nc.sbuf_tensor("arena", [128, ARENA_BYTES // 4], F32))
        psum = es.enter_context(nc.psum_tensor("psum", [128, 4096], F32))
        for nm, eng in (("pe", nc.tensor), ("act", nc.scalar), ("dve", nc.vector), ("pool", nc.gpsimd), ("sp", nc.sync)):
            B.reg_engine(nm, eng, es.enter_context(nc.semaphore("s_" + nm)))

        def newdsem(name):
            s = es.enter_context(nc.semaphore("d_" + name))
            return {"sem": s, "val": 0, "key": "d_" + name}

        off = [0]

        def alloc(nbytes):
            o = off[0]
            off[0] = (o + nbytes + 63) // 64 * 64
            assert off[0] <= ARENA_BYTES, off[0]
            return o

        def sb(o, dtype, n, shape=None):
            es_ = 4 if dtype == F32 else 2
            assert o % 4 == 0 and (n * es_) % 4 == 0, (o, n)
            ap = arena[:, o // 4:(o + n * es_) // 4]
            if dtype != F32:
                ap = ap.bitcast(dtype)
            if shape is not None:
                names = " ".join("a%d" % i for i in range(len(shape)))
                kw = {"a%d" % i: s for i, s in enumerate(shape)}
                ap = ap.rearrange("p (%s) -> p %s" % (names, names), **kw)
            return B.mk(ap, "S", o, o + n * es_)

        def ps(bank, n, dtype=F32, colo=0, shape=None):
            es_ = 4 if dtype == F32 else 2
            lo = bank * 2048 + colo * 4
            ap = psum[:, lo // 4:(lo + n * es_) // 4]
            if dtype != F32:
                ap = ap.bitcast(dtype)
            if shape is not None:
                names = " ".join("a%d" % i for i in range(len(shape)))
                kw = {"a%d" % i: s for i, s in enumerate(shape)}
                ap = ap.rearrange("p (%s) -> p %s" % (names, names), **kw)
            return B.mk(ap, "P", bank * 2048, bank * 2048 + 2048)

        o_res = alloc(8 * NT * 4)
        RES = [[sb(o_res + (c * NT + t0) * 4, F32, N) for (t0, N, _) in GROUPS] for c in range(8)]
        o_c = alloc(4 * 128 * 2)
        CBF = sb(o_c, BF16, 512, [4, 128])
        ident, pm, mask0, mask2 = (CBF.ap[:, i, :] for i in range(4))
        ONES = sb(alloc(256), BF16, 128)
        EPS_T = sb(alloc(64), F32, 4)
        CT = sb(alloc(64), F32, 16, [8, 2])
        ST = sb(alloc(32), BF16, 16, [8, 2])
        CTMP = sb(alloc(64), F32, 16, [8, 2])
        MODV = sb(alloc(576), F32, 144, [72, 2])
        BMOD = sb(alloc(288), F32, 72)
        NPRE = sb(alloc(96), F32, 24)
        NPOST = sb(alloc(96), F32, 24)
        GM = sb(alloc(192), F32, 48, [3, 2, 8])
        GG = sb(alloc(192), F32, 48, [3, 2, 8])
        WST = sb(alloc(1024), BF16, 512, [4, 128])
        WSTF = sb(alloc(2048), F32, 512)
        BSBC = sb(alloc(1024), F32, 256, [2, 128])
        VGAIN = sb(alloc(1024), F32, 256)
        SUBLN = sb(alloc(512), F32, 128)
        DLAM = sb(alloc(1024), F32, 256, [4, 64])
        SINK = sb(alloc(64), F32, 4)
        SMALL = sb(alloc(128), F32, 32)
        small_lo = SMALL.lo

        def small(i, n=1):
            return sb(small_lo + 4 * i, F32, n)

        TF = [sb(alloc(2048), F32, 512) for _ in range(6)]
        TB = [sb(alloc(1024), BF16, 512) for _ in range(4)]
        RSP = [sb(alloc(2048), F32, 512) for _ in range(2)]
        rsi = [0]

        def rsp():
            rsi[0] ^= 1
            return RSP[rsi[0]]
        LS = small(0, 2)
        NLAM_T = small(2)
        SSV = small(4, 4)
        RR = small(12, 2)
        SSB = small(16)
        DEN = small(20, 4)
        tfi = [0]
        tbi = [0]

        def tf():
            tfi[0] = (tfi[0] + 1) % len(TF)
            return TF[tfi[0]]

        def tb():
            tbi[0] = (tbi[0] + 1) % len(TB)
            return TB[tbi[0]]

        WS = [sb(alloc(4096), BF16, 2048, [8, 256]) for _ in range(4)]
        WS_sem = [newdsem("ws%d" % i) for i in range(4)]
        wsi = [0]

        def wslot():
            wsi[0] = (wsi[0] + 1) % 4
            return wsi[0]

        scratch0 = off[0]
        HMAX = 1024
        o_y = alloc(8 * HMAX * 2)
        o_h = alloc(NJ * HMAX * 2)
        o_wd = [alloc(NJ * 128 * 2) for _ in range(2)]
        WD = [sb(o, BF16, NJ * 128, [NJ, 128]) for o in o_wd]
        WD_sem = [newdsem("wd%d" % i) for i in range(2)]
        ffn_end = off[0]
        o_o_extra = alloc(8 * HMAX * 4 - 8 * HMAX * 2)
        o_o = None
        ffn_end = off[0]
        def o_off(c, loff):
            if c < 4:
                return o_y + (c * HMAX + loff) * 4
            return o_o_extra + ((c - 4) * HMAX + loff) * 4
        off[0] = scratch0
        o_ktb = alloc(4 * NT * 2)
        KTB = [[sb(o_ktb + (h * NT + kt * 128) * 2, BF16, 128) for kt in range(18)] for h in range(4)]
        o_ktc = alloc(NT * 2)
        KTC = [sb(o_ktc + kt * 128 * 2, BF16, 128) for kt in range(18)]
        o_vb = alloc(18 * 4 * 130 * 2)
        VB = [sb(o_vb + kt * 4 * 130 * 2, BF16, 4 * 130, [4, 130]) for kt in range(18)]
        VBALL = sb(o_vb, BF16, 18 * 4 * 130, [72, 130])
        o_vc = alloc(18 * 2 * 66 * 2)
        VC = [sb(o_vc + kt * 2 * 66 * 2, BF16, 2 * 66, [2, 66]) for kt in range(18)]
        VCALL = sb(o_vc, BF16, 18 * 2 * 66, [36, 66])
        TAB = sb(alloc(4096), F32, 1024, [2, 512])
        TAB_sem = newdsem("tab")
        o_ax = alloc(8 * 512 * 2)
        o_qb = alloc(4 * 512 * 2)
        o_qc = alloc(2 * 512 * 2)
        o_u = alloc(2 * 512 * 4)
        o_vp = alloc(4 * 512 * 2)
        mix_o_lo = o_ax
        assert off[0] - o_ax >= 8 * 512 * 4
        o_om = alloc(8 * 512 * 2)
        PT = [sb(alloc(1024), BF16, 512) for _ in range(3)]
        mix_end = off[0]
        pti = [0]

        def ptile():
            pti[0] = (pti[0] + 1) % 3
            return PT[pti[0]]

        off[0] = scratch0
        WM = [sb(alloc(8192), BF16, 4096, [8, 512]) for _ in range(2)]
        WM_sem = [newdsem("wm%d" % i) for i in range(2)]
        print("arena: persistent %d, ffn_end %d, mix_end %d" % (scratch0, ffn_end, mix_end))

        def psbank(b, n=512):
            return ps(b, n)
        PS_SS = psbank(0)
        PG = [psbank(0), psbank(1)]
        PU = [psbank(2), psbank(3)]
        PO = [psbank(4), psbank(5)]
        PSS2 = [psbank(6), psbank(7)]
        PGEN = [psbank(1), psbank(2)]
        PGEN_BF = [ps(1, 1024, BF16), ps(2, 1024, BF16)]
        PST = [psbank(3), psbank(4)]
        pgi = [0]
        psti = [0]

        def pgen():
            pgi[0] ^= 1
            return pgi[0]

        def pst():
            psti[0] ^= 1
            return PST[psti[0]]

        ACCB = []
        for a in range(8):
            ACCB.append(ps(5 + a // 3, 130, F32, (a % 3) * 130))
        ACCC = ps(5, 4 * 66, F32, 0, [4, 66])
        PMOD = ps(7, 144, F32, 0, [72, 2])

        dsem_misc = newdsem("misc")
        dsem_par = [newdsem("par0"), newdsem("par1")]
        dsem_res = newdsem("res")
        dsem_out = newdsem("out")

        def mm(out_t, lhsT, rhs, start, stop, R, inc=None, out_ap=None, skip=False):
            oap = out_ap if out_ap is not None else out_t.ap
            kw = {"skip_group_check": True} if skip else {}
            return B.op("pe", lambda: nc.tensor.matmul(oap, lhsT, rhs, start=start, stop=stop, **kw),
                        R=R, W=[out_t], inc=(stop if inc is None else inc))

        def act(out_t, in_ap, func, R, out_ap=None, **kw):
            oap = out_ap if out_ap is not None else out_t.ap
            return B.op("act", lambda: nc.scalar.activation(oap, in_ap, func, **kw), R=R, W=[out_t])

        def fixmarks(tiles_, ds):
            for t_ in tiles_:
                t_.w = (ds["key"], ds["sem"], ds["val"])

        def dve(fn, R, W):
            return B.op("dve", fn, R=R, W=W)

        for c in range(8):
            for gi, (t0, N, _) in enumerate(GROUPS):
                pass
        RES_ALL = [sb(o_res + c * NT * 4, F32, NT) for c in range(8)]
        for c in range(8):
            B.dma("sp", RES_ALL[c], xT_d[:, c, :], dsem_res)
        fixmarks(RES_ALL, dsem_res)
        dsem_misc2 = newdsem("misc2")
        B.dma("pool", CBF, cbf_d, dsem_misc2, out_ap=arena[:, o_c // 4:(o_c + 1024) // 4].bitcast(BF16))
        B.dma("sp", CT, cT_d, dsem_misc, out_ap=arena[:, CT.lo // 4:CT.lo // 4 + 16])
        B.op("pool", lambda: nc.gpsimd.memset(ONES.ap, 1.0), R=[], W=[ONES])
        B.op("pool", lambda: nc.gpsimd.memset(EPS_T.ap, EPS), R=[], W=[EPS_T])
        B.op("pool", lambda: nc.gpsimd.memset(EPS_T.ap[:, 1:2], 4 * EPS), R=[], W=[EPS_T])
        B.op("pool", lambda: nc.gpsimd.memset(EPS_T.ap[:, 2:3], 1.0), R=[], W=[EPS_T])
        act(CTMP, CT.ap, AF.Tanh, [CT], scale=0.5)
        dve(lambda: nc.vector.scalar_tensor_tensor(CTMP.ap, CTMP.ap, 1.0, CT.ap, ALU.add, ALU.mult), [CT, CTMP], [CTMP])
        B.op("act", lambda: nc.scalar.mul(ST.ap, CTMP.ap, 0.5), R=[CTMP], W=[ST])

        def load_layer_params(l):
            ds = dsem_par[l % 2]
            for t, d_ in ((BMOD, bmod_d[l]), (NPRE, npre_d[l]), (NPOST, npost_d[l]), (WSTF, wsT_d[l]),
                          (VGAIN, vgain_d[l]), (SUBLN, subln_d[l]), (SINK, sink_d[l])):
                B.dma("sp", t, d_, ds)
            B.dma("sp", BSBC, bsbc_d[l], ds, out_ap=arena[:, BSBC.lo // 4:BSBC.lo // 4 + 256])
            B.dma("sp", DLAM, dlam_d[l], ds, out_ap=arena[:, DLAM.lo // 4:DLAM.lo // 4 + 256])
            fixmarks([BMOD, NPRE, NPOST, WSTF, VGAIN, SUBLN, SINK, BSBC, DLAM], ds)

        def layer_prologue(l):
            lam_init = 0.8 - 0.6 * math.exp(-0.3 * l)
            load_layer_params(l)
            def ldm(s):
                B.dma("pool", WM[s % 2], wmod_d[l, s], WM_sem[s % 2],
                      out_ap=arena[:, WM[s % 2].lo // 4:WM[s % 2].lo // 4 + 2048].bitcast(BF16))
            ldm(0)
            for s in range(18):
                if s + 1 < 18:
                    ldm(s + 1)
                w = WM[s % 2]
                for mc in range(4):
                    ic = s * 4 + mc
                    for k in range(8):
                        mm(PMOD, w.ap[:, k, mc * 128:(mc + 1) * 128], ST.ap[:, k, :], k == 0, k == 7,
                           [w, ST], out_ap=PMOD.ap[:, ic, :])
            for j in range(2):
                dve(lambda j=j: nc.vector.tensor_tensor(MODV.ap[:, :, j], PMOD.ap[:, :, j], BMOD.ap, ALU.add),
                    [PMOD, BMOD], [MODV])
            for sub in range(3):
                wgt = 1.0 if sub == 1 else 0.5
                for j in range(2):
                    dve(lambda sub=sub, j=j: nc.vector.scalar_tensor_tensor(
                        GM.ap[:, sub, j, :], MODV.ap[:, (3 * sub + 1) * 8:(3 * sub + 1) * 8 + 8, j], 1.0,
                        NPRE.ap[:, sub * 8:sub * 8 + 8], ALU.add, ALU.mult), [MODV, NPRE], [GM])
                    dve(lambda sub=sub, j=j, wgt=wgt: nc.vector.scalar_tensor_tensor(
                        GG.ap[:, sub, j, :], MODV.ap[:, (3 * sub + 2) * 8:(3 * sub + 2) * 8 + 8, j], wgt,
                        NPOST.ap[:, sub * 8:sub * 8 + 8], ALU.mult, ALU.mult), [MODV, NPOST], [GG])
            act(WST, WSTF.ap.rearrange("p (h q) -> p h q", h=4), AF.Copy, [WSTF])
            B.op("act", lambda: nc.scalar.mul(SUBLN.ap, SUBLN.ap, 1.0 - lam_init), R=[SUBLN], W=[SUBLN])
            act(SINK, SINK.ap, AF.Exp, [SINK])
            lt = tf()
            dl4 = DLAM.ap.rearrange("p (a b) d -> p a b d", a=2)
            dve(lambda: nc.vector.tensor_tensor(lt.ap[:, 0:128].rearrange("p (a b) -> p a b", a=2),
                                                dl4[:, :, 0, :], dl4[:, :, 1, :], ALU.mult), [DLAM], [lt])
            dve(lambda: nc.vector.tensor_reduce(LS.ap, lt.ap[:, 0:128].rearrange("p (a b) -> p a b", a=2), AX.X, ALU.add),
                [lt], [LS])
            act(LS, LS.ap, AF.Exp, [LS])
            NLAM = NLAM_T
            dve(lambda: nc.vector.scalar_tensor_tensor(NLAM.ap, LS.ap[:, 1:2], -lam_init, LS.ap[:, 0:1], ALU.add, ALU.subtract),
                [LS], [NLAM])
            return NLAM

        def rstd_from(ps_t, N, scale, eps_ap):
            sd = rsp()
            act(sd, ps_t.ap[:, :N], AF.Ln, [ps_t, EPS_T], out_ap=sd.ap[:, :N], scale=scale, bias=eps_ap)
            act(sd, sd.ap[:, :N], AF.Exp, [sd], out_ap=sd.ap[:, :N], scale=-0.5)
            return sd

        def prenorm(sub, gi, outs, outaps):
            t0, N, j = GROUPS[gi]
            for c in range(8):
                sq = tb()
                act(sq, RES[c][gi].ap, AF.Square, [RES[c][gi]], out_ap=sq.ap[:, :N])
                mm(PS_SS, ONES.ap, sq.ap[:, :N], c == 0, c == 7, [ONES, sq], out_ap=PS_SS.ap[:, :N], inc=True)
            rs = rstd_from(PS_SS, N, 1.0 / D, EPS_T.ap[:, 0:1])
            for c in range(8):
                t = tf()
                dve(lambda c=c, t=t: nc.vector.tensor_tensor(t.ap[:, :N], RES[c][gi].ap, rs.ap[:, :N], ALU.mult),
                    [RES[c][gi], rs], [t])
                act(outs[c], t.ap[:, :N], AF.Identity, [t, GM, MODV], out_ap=outaps[c],
                    scale=GM.ap[:, sub, j, c:c + 1], bias=MODV.ap[:, (3 * sub) * 8 + c, j:j + 1])

        def postnorm(sub, gi, ps_ss, o_tiles, o_aps):
            t0, N, j = GROUPS[gi]
            rs = rstd_from(ps_ss, N, 1.0 / D, EPS_T.ap[:, 0:1])
            for c in range(8):
                t = tf()
                dve(lambda c=c, t=t: nc.vector.tensor_tensor(t.ap[:, :N], o_aps[c], rs.ap[:, :N], ALU.mult),
                    [o_tiles[c], rs], [t])
                dve(lambda c=c, t=t: nc.vector.scalar_tensor_tensor(
                    RES[c][gi].ap, t.ap[:, :N], GG.ap[:, sub, j, c:c + 1], RES[c][gi].ap, ALU.mult, ALU.add),
                    [t, GG, RES[c][gi]], [RES[c][gi]])

        def wload(slot, dram_ap):
            B.dma("pool", WS[slot], dram_ap, WS_sem[slot],
                  out_ap=arena[:, WS[slot].lo // 4:WS[slot].lo // 4 + 1024].bitcast(BF16))

        class Streamer:
            def __init__(self, aps):
                self.aps = aps
                self.issued = 0
                self.cur = 0
                self.slots = []

            def get(self):
                while self.issued < len(self.aps) and self.issued <= self.cur + 2:
                    s_ = wslot()
                    wload(s_, self.aps[self.issued])
                    self.slots.append(s_)
                    self.issued += 1
                s_ = self.slots[self.cur]
                self.cur += 1
                return WS[s_]

        def ffn(l, f, sub, gis):
            halves = [[0, 1], [2, 3], [4]] if 0 in gis else [[1, 2], [3, 4]]
            pending_first = [None]
            for ih, half in enumerate(halves):
                loff = {}
                o_ = 0
                for g in half:
                    loff[g] = o_
                    o_ += GROUPS[g][1]
                Y = {g: [sb(o_y + (c * HMAX + loff[g]) * 2, BF16, GROUPS[g][1]) for c in range(8)] for g in half}
                H = {g: [sb(o_h + (jj * HMAX + loff[g]) * 2, BF16, GROUPS[g][1]) for jj in range(NJ)] for g in half}
                O = {g: [sb(o_off(c, loff[g]), F32, GROUPS[g][1]) for c in range(8)] for g in half}
                for g in half:
                    prenorm(sub, g, Y[g], [y.ap for y in Y[g]])
                def ld1(q):
                    a, b_ = wslot(), wslot()
                    wload(a, wg_d[l, f, q])
                    wload(b_, wu_d[l, f, q])
                    return a, b_
                def ld2(c):
                    s = c % 2
                    B.dma("pool", WD[s], wd_d[l, f, c], WD_sem[s],
                          out_ap=arena[:, WD[s].lo // 4:WD[s].lo // 4 + NJ * 64].bitcast(BF16))
                nxt = pending_first[0] if pending_first[0] is not None else ld1(0)
                pending_first[0] = None
                for q in range(11):
                    if q == 9:
                        ld2(0)
                        ld2(1)
                    cur = nxt
                    if q + 1 < 11:
                        nxt = ld1(q + 1)
                    wgs, wus = WS[cur[0]], WS[cur[1]]
                    for g in half:
                        N = GROUPS[g][1]
                        for jj in range(2):
                            jh = 2 * q + jj
                            bi = pgen()
                            pg_, pu_ = PG[bi], PU[bi]
                            for k in range(8):
                                mm(pg_, wgs.ap[:, k, jj * 128:(jj + 1) * 128], Y[g][k].ap, k == 0, k == 7,
                                   [wgs, Y[g][k]], out_ap=pg_.ap[:, :N])
                            for k in range(8):
                                mm(pu_, wus.ap[:, k, jj * 128:(jj + 1) * 128], Y[g][k].ap, k == 0, k == 7,
                                   [wus, Y[g][k]], out_ap=pu_.ap[:, :N])
                            th = tf()
                            act(th, pg_.ap[:, :N], AF.Tanh, [pg_], out_ap=th.ap[:, :N], scale=0.5)
                            dve(lambda th=th, pg_=pg_, N=N: nc.vector.scalar_tensor_tensor(
                                th.ap[:, :N], th.ap[:, :N], 1.0, pg_.ap[:, :N], ALU.add, ALU.mult), [th, pg_], [th])
                            dve(lambda th=th, pu_=pu_, N=N, g=g, jh=jh: nc.vector.scalar_tensor_tensor(
                                H[g][jh].ap, th.ap[:, :N], 0.5, pu_.ap[:, :N], ALU.mult, ALU.mult), [th, pu_], [H[g][jh]])
                for c in range(8):
                    if c == 5 and ih + 1 < len(halves):
                        pending_first[0] = ld1(0)
                    wdt = WD[c % 2]
                    for ig, g in enumerate(half):
                        N = GROUPS[g][1]
                        po = PO[pgen()]
                        for jh in range(NJ):
                            mm(po, wdt.ap[:, jh, :], H[g][jh].ap, jh == 0, jh == NJ - 1, [wdt, H[g][jh]],
                               out_ap=po.ap[:, :N])
                        act(O[g][c], po.ap[:, :N], AF.Copy, [po])
                        sq = tb()
                        act(sq, po.ap[:, :N], AF.Square, [po], out_ap=sq.ap[:, :N])
                        mm(PSS2[ig], ONES.ap, sq.ap[:, :N], c == 0, c == 7, [ONES, sq], out_ap=PSS2[ig].ap[:, :N], inc=True)
                    if c + 2 < 8:
                        ld2(c + 2)
                for ig, g in enumerate(half):
                    postnorm(sub, g, PSS2[ig], O[g], [o.ap for o in O[g]])

        def rope(src_ps, N, out_t, out_ap, tabs):
            kb = tb()
            act(kb, src_ps.ap[:, :N], AF.Copy, [src_ps], out_ap=kb.ap[:, :N])
            bi = pgen()
            pr = PGEN[bi]
            mm(pr, pm, kb.ap[:, :N], True, True, [CBF, kb], out_ap=pr.ap[:, :N])
            t1 = tf()
            dve(lambda: nc.vector.tensor_tensor(t1.ap[:, :N], kb.ap[:, :N], TAB.ap[:, 0, :N], ALU.mult), [kb, TAB], [t1])
            t2 = tf()
            dve(lambda: nc.vector.tensor_tensor(t2.ap[:, :N], pr.ap[:, :N], TAB.ap[:, 1, :N], ALU.mult), [pr, TAB], [t2])
            dve(lambda: nc.vector.tensor_tensor(out_ap, t1.ap[:, :N], t2.ap[:, :N], ALU.add), [t1, t2], [out_t])

        def load_tab(gi):
            t0, N, _ = GROUPS[gi]
            s0 = t0 - NCTX
            B.dma("sp", TAB, cos_d[:, s0:s0 + 512], TAB_sem, out_ap=TAB.ap[:, 0, :])
            B.dma("sp", TAB, sin_d[:, s0:s0 + 512], TAB_sem, out_ap=TAB.ap[:, 1, :])

        def gelu2(src_ps_ap, src_t, n, out_t=None, out_ap=None):
            sq = tf()
            act(sq, src_ps_ap, AF.Square, [src_t], out_ap=sq.ap[:, :n])
            dve(lambda: nc.vector.scalar_tensor_tensor(sq.ap[:, :n], sq.ap[:, :n], 0.044715, src_ps_ap, ALU.mult, ALU.mult),
                [sq, src_t], [sq])
            dve(lambda: nc.vector.tensor_tensor(sq.ap[:, :n], sq.ap[:, :n], src_ps_ap, ALU.add), [sq, src_t], [sq])
            th = tf()
            act(th, sq.ap[:, :n], AF.Tanh, [sq], out_ap=th.ap[:, :n], scale=GC1)
            if out_t is None:
                out_t, out_ap = th, th.ap[:, :n]
            dve(lambda: nc.vector.scalar_tensor_tensor(out_ap, th.ap[:, :n], 1.0, src_ps_ap, ALU.add, ALU.mult),
                [th, src_t], [out_t])
            return out_t

        def mixer(l, last, NLAM):
            AXT = [sb(o_ax + c * 512 * 2, BF16, 512) for c in range(8)]
            for q4 in range(2):
                act(VBALL, ONES.ap[:, 0:72].rearrange("p (a b) -> p a b", b=2), AF.Copy, [ONES],
                    out_ap=VBALL.ap[:, q4 * 36:(q4 + 1) * 36, 128:130])
            act(VCALL, ONES.ap[:, 0:72].rearrange("p (a b) -> p a b", b=2), AF.Copy, [ONES], out_ap=VCALL.ap[:, :, 64:66])
            VPALL0 = sb(o_vp, BF16, 2048)
            B.op("act", lambda: nc.scalar.memzero(VPALL0.ap), R=[], W=[VPALL0])
            plist = []
            if "p1" in stages:
                for gi_ in range(5):
                    plist += [wkv_d[l, pc_] for pc_ in range(5)]
            for gi_ in range(5):
                if last and gi_ == 0:
                    continue
                if "p2q" in stages:
                    plist += [wq_d[l, pc_] for pc_ in range(5)]
                plist += [wout_d[l, pc_] for pc_ in range(4)]
            STR = Streamer(plist)
            for gi in (range(5) if "p1" in stages else []):
                t0, N, j = GROUPS[gi]
                lat = gi > 0
                if lat:
                    load_tab(gi)
                prenorm(1, gi, AXT, [a.ap[:, :N] for a in AXT])
                nt_ = N // 128
                for pc in range(5):
                    w = STR.get()
                    if pc < 2 or pc == 2:
                        chunks = [(0, 2 * pc), (1, 2 * pc + 1)] if pc < 2 else [(0, 4)]
                        for cc, ch in chunks:
                            bi = pgen()
                            pp = PGEN[bi]
                            for k in range(8):
                                mm(pp, w.ap[:, k, cc * 128:(cc + 1) * 128], AXT[k].ap[:, :N], k == 0, k == 7,
                                   [w, AXT[k]], out_ap=pp.ap[:, :N])
                            kt0 = t0 // 128
                            if ch < 4:
                                dst = [KTB[ch][kt0 + i] for i in range(nt_)]
                                dap = arena[:, (o_ktb + (ch * NT + t0) * 2) // 4:(o_ktb + (ch * NT + t0 + N) * 2) // 4].bitcast(BF16)
                            else:
                                dst = [KTC[kt0 + i] for i in range(nt_)]
                                dap = arena[:, (o_ktc + t0 * 2) // 4:(o_ktc + (t0 + N) * 2) // 4].bitcast(BF16)
                            big = B.mk(dap, "S", dst[0].lo, dst[-1].hi)
                            if lat:
                                rope(pp, N, big, dap, None)
                            else:
                                act(big, pp.ap[:, :N], AF.Copy, [pp])
                    if pc >= 2:
                        for tt in range(nt_):
                            kt = t0 // 128 + tt
                            bi = pgen()
                            pv = PGEN[bi]
                            c0 = 128 if pc == 2 else 0
                            ncol = 128 if pc == 2 else 256
                            for k in range(8):
                                mm(pv, AXT[k].ap[:, tt * 128:(tt + 1) * 128], w.ap[:, k, c0:c0 + ncol], k == 0, k == 7,
                                   [w, AXT[k]], out_ap=pv.ap[:, :ncol])
                            if pc == 2:
                                act(VC[kt], pv.ap[:, 0:128].rearrange("p (a b) -> p a b", a=2), AF.Copy, [pv],
                                    out_ap=VC[kt].ap[:, :, 0:64])
                            else:
                                h0 = 2 * (pc - 3)
                                act(VB[kt], pv.ap[:, 0:256].rearrange("p (a b) -> p a b", a=2), AF.Copy, [pv],
                                    out_ap=VB[kt].ap[:, h0:h0 + 2, 0:128])
            QB = [sb(o_qb + h * 512 * 2, BF16, 512) for h in range(4)]
            QC = [sb(o_qc + g * 512 * 2, BF16, 512) for g in range(2)]
            UT = [sb(o_u + hp * 512 * 4, F32, 512) for hp in range(2)]
            VP = [sb(o_vp + tt * 512 * 2, BF16, 512, [4, 128]) for tt in range(4)]
            VPALL = sb(o_vp, BF16, 2048)
            OM = [sb(o_om + c * 512 * 2, BF16, 512) for c in range(8)]
            OO = [sb(mix_o_lo + c * 512 * 4, F32, 512) for c in range(8)]
            for gi in range(5):
                if last and gi == 0:
                    continue
                t0, N, j = GROUPS[gi]
                lat = gi > 0
                nt_ = N // 128
                kt0 = t0 // 128
                if lat:
                    load_tab(gi)
                prenorm(1, gi, AXT, [a.ap[:, :N] for a in AXT])
                for pc in (range(5) if "p2q" in stages else []):
                    w = STR.get()
                    if pc < 3:
                        for cc in range(2):
                            bi = pgen()
                            pp = PGEN[bi]
                            for k in range(8):
                                mm(pp, w.ap[:, k, cc * 128:(cc + 1) * 128], AXT[k].ap[:, :N], k == 0, k == 7,
                                   [w, AXT[k]], out_ap=pp.ap[:, :N])
                            dst = QB[2 * pc + cc] if pc < 2 else QC[cc]
                            if lat:
                                rope(pp, N, dst, dst.ap[:, :N], None)
                            else:
                                act(dst, pp.ap[:, :N], AF.Copy, [pp], out_ap=dst.ap[:, :N])
                    elif pc == 3:
                        for cc in range(2):
                            bi = pgen()
                            pp = PGEN[bi]
                            for k in range(8):
                                mm(pp, w.ap[:, k, cc * 128:(cc + 1) * 128], AXT[k].ap[:, :N], k == 0, k == 7,
                                   [w, AXT[k]], out_ap=pp.ap[:, :N])
                            gelu2(pp.ap[:, :N], pp, N, out_t=UT[cc], out_ap=UT[cc].ap[:, :N])
                    else:
                        for tt in range(nt_):
                            bi = pgen()
                            pv = PGEN[bi]
                            for k in range(8):
                                mm(pv, AXT[k].ap[:, tt * 128:(tt + 1) * 128], w.ap[:, k, :], k == 0, k == 7,
                                   [w, AXT[k]], out_ap=pv.ap[:, :256])
                            g2 = gelu2(pv.ap[:, :256], pv, 256)
                            sqv = tf()
                            dve(lambda g2=g2, sqv=sqv: nc.vector.tensor_tensor(sqv.ap[:, :256], g2.ap[:, :256], g2.ap[:, :256], ALU.mult),
                                [g2], [sqv])
                            ssv = SSV
                            dve(lambda sqv=sqv, ssv=ssv: nc.vector.tensor_reduce(
                                ssv.ap, sqv.ap[:, :256].rearrange("p (h c) -> p h c", h=4), AX.X, ALU.add), [sqv], [ssv])
                            act(ssv, ssv.ap, AF.Ln, [ssv, EPS_T], scale=1.0 / 64, bias=EPS_T.ap[:, 1:2])
                            act(ssv, ssv.ap, AF.Exp, [ssv], scale=-0.5)
                            for hh in range(4):
                                dve(lambda g2=g2, ssv=ssv, hh=hh, tt=tt: nc.vector.scalar_tensor_tensor(
                                    VP[tt].ap[:, hh, (hh % 2) * 64:(hh % 2) * 64 + 64], g2.ap[:, hh * 64:(hh + 1) * 64],
                                    ssv.ap[:, hh:hh + 1], VGAIN.ap[:, hh * 64:(hh + 1) * 64], ALU.mult, ALU.mult),
                                    [g2, ssv, VGAIN], [VP[tt]])
                            for hp in range(2):
                                bi2 = pgen()
                                pm_ = PGEN[bi2]
                                mm(pm_, VP[tt].ap[:, 2 * hp, :], WST.ap[:, 2 * hp, :], True, False, [VP[tt], WST],
                                   out_ap=pm_.ap[:, :128])
                                mm(pm_, VP[tt].ap[:, 2 * hp + 1, :], WST.ap[:, 2 * hp + 1, :], False, True, [VP[tt], WST],
                                   out_ap=pm_.ap[:, :128])
                                t = tf()
                                dve(lambda t=t, pm_=pm_, hp=hp: nc.vector.tensor_tensor(t.ap[:, :128], pm_.ap[:, :128], BSBC.ap[:, hp, :], ALU.add),
                                    [pm_, BSBC], [t])
                                dve(lambda t=t, hp=hp, tt=tt: nc.vector.scalar_tensor_tensor(
                                    OM[hp].ap[:, tt * 128:(tt + 1) * 128], UT[hp].ap[:, tt * 128:(tt + 1) * 128], 0.5, t.ap[:, :128],
                                    ALU.mult, ALU.mult), [t, UT[hp]], [OM[hp]])
                kts = list(range(18)) if lat else [0, 1]
                for h in (range(4) if "B" in stages else []):
                    first_in_bank = set()
                    for ik, kt in enumerate(kts):
                        for m in range(2):
                            st_ = pst()
                            mm(st_, KTB[h][kt].ap[m * 64:(m + 1) * 64, :], QB[h].ap[m * 64:(m + 1) * 64, :N], True, True,
                               [KTB[h][kt], QB[h]], out_ap=st_.ap[:, :N])
                            p_ = ptile()
                            act(p_, st_.ap[:, :N], AF.Exp, [st_], out_ap=p_.ap[:, :N], scale=0.125)
                            for qt in range(nt_):
                                a = qt * 2 + m
                                bank = a // 3
                                st_flag = (ik == 0) and (bank not in first_in_bank)
                                first_in_bank.add(bank)
                                mm(ACCB[a], p_.ap[:, qt * 128:(qt + 1) * 128], VB[kt].ap[:, h, 0:130], st_flag,
                                   ik == len(kts) - 1, [p_, VB[kt]], skip=True, inc=(ik == len(kts) - 1))
                    for qt in range(nt_):
                        a1, a2 = ACCB[qt * 2], ACCB[qt * 2 + 1]
                        rr = RR
                        dve(lambda a1=a1, rr=rr: nc.vector.reciprocal(rr.ap[:, 0:1], a1.ap[:, 128:129]), [a1], [rr])
                        dve(lambda a2=a2, rr=rr: nc.vector.reciprocal(rr.ap[:, 1:2], a2.ap[:, 128:129]), [a2, rr], [rr])
                        dve(lambda rr=rr: nc.vector.tensor_tensor(rr.ap[:, 1:2], rr.ap[:, 1:2], NLAM.ap, ALU.mult), [rr, NLAM], [rr])
                        o1 = tf()
                        dve(lambda a1=a1, rr=rr, o1=o1: nc.vector.tensor_scalar(o1.ap[:, :128], a1.ap[:, 0:128], rr.ap[:, 0:1], None, ALU.mult),
                            [a1, rr], [o1])
                        dve(lambda a2=a2, rr=rr, o1=o1: nc.vector.scalar_tensor_tensor(
                            o1.ap[:, :128], a2.ap[:, 0:128], rr.ap[:, 1:2], o1.ap[:, :128], ALU.mult, ALU.add), [a2, rr, o1], [o1])
                        sqt = tf()
                        ssb = SSB
                        act(sqt, o1.ap[:, :128], AF.Square, [o1], out_ap=sqt.ap[:, :128])
                        dve(lambda sqt=sqt, ssb=ssb: nc.vector.tensor_reduce(ssb.ap, sqt.ap[:, :128], AX.X, ALU.add), [sqt], [ssb])
                        act(ssb, ssb.ap, AF.Ln, [ssb, EPS_T], scale=1.0 / 128, bias=EPS_T.ap[:, 0:1])
                        act(ssb, ssb.ap, AF.Exp, [ssb], scale=-0.5)
                        ob = tb()
                        dve(lambda o1=o1, ssb=ssb, ob=ob: nc.vector.scalar_tensor_tensor(
                            ob.ap[:, :128], o1.ap[:, :128], ssb.ap, SUBLN.ap, ALU.mult, ALU.mult), [o1, ssb, SUBLN], [ob])
                        bi = pgen()
                        ptp = PGEN[bi]
                        mm(ptp, ob.ap[:, :128], ident, True, True, [ob, CBF], out_ap=ptp.ap[:, :128])
                        act(OM[2 + h], ptp.ap[:, :128], AF.Copy, [ptp], out_ap=OM[2 + h].ap[:, qt * 128:(qt + 1) * 128])
                for qt in (range(nt_) if "C" in stages else []):
                    if lat:
                        n = (t0 - NCTX) // 128 + qt
                        kl = [(0, None), (1, None)]
                        if n - 1 >= 0:
                            kl.append((n - 1 + 2, mask0))
                        kl.append((n + 2, None))
                        if n + 1 <= 15:
                            kl.append((n + 1 + 2, mask2))
                    else:
                        kl = [(0, None), (1, None)]
                    for ik, (kt, msk) in enumerate(kl):
                        for hd in range(4):
                            kvh, g_ = hd // 2, hd % 2
                            st_ = PST[kvh]
                            mm(st_, KTC[kt].ap[kvh * 64:(kvh + 1) * 64, :], QC[g_].ap[kvh * 64:(kvh + 1) * 64, qt * 128:(qt + 1) * 128],
                               True, True, [KTC[kt], QC[g_]], out_ap=st_.ap[:, g_ * 128:(g_ + 1) * 128], skip=True, inc=True)
                        p_ = ptile()
                        for kvh in range(2):
                            act(p_, PST[kvh].ap[:, 0:256], AF.Exp, [PST[kvh]], out_ap=p_.ap[:, kvh * 256:(kvh + 1) * 256], scale=0.125)
                        if msk is not None:
                            for hh in range(4):
                                dve(lambda p_=p_, msk=msk, hh=hh: nc.vector.tensor_tensor(
                                    p_.ap[:, hh * 128:(hh + 1) * 128], p_.ap[:, hh * 128:(hh + 1) * 128], msk, ALU.mult), [p_, CBF], [p_])
                        for hd in range(4):
                            kvh = hd // 2
                            mm(ACCC, p_.ap[:, hd * 128:(hd + 1) * 128], VC[kt].ap[:, kvh, 0:66], (ik == 0 and hd == 0),
                               ik == len(kl) - 1, [p_, VC[kt]], out_ap=ACCC.ap[:, hd, :], skip=True,
                               inc=(ik == len(kl) - 1 and hd == 3))
                    den = DEN
                    dve(lambda den=den: nc.vector.tensor_tensor(den.ap, ACCC.ap[:, :, 64], SINK.ap, ALU.add), [ACCC, SINK], [den])
                    act(den, den.ap, AF.Ln, [den])
                    act(den, den.ap, AF.Exp, [den], scale=-1.0)
                    oc = tb()
                    for hh in range(4):
                        dve(lambda den=den, oc=oc, hh=hh: nc.vector.tensor_scalar(
                            oc.ap[:, hh * 64:(hh + 1) * 64], ACCC.ap[:, hh, 0:64], den.ap[:, hh:hh + 1], None, ALU.mult),
                            [ACCC, den], [oc])
                    for i in range(2):
                        bi = pgen()
                        ptp = PGEN[bi]
                        mm(ptp, oc.ap[:, i * 128:(i + 1) * 128], ident, True, True, [oc, CBF], out_ap=ptp.ap[:, :128])
                        act(OM[6 + i], ptp.ap[:, :128], AF.Copy, [ptp], out_ap=OM[6 + i].ap[:, qt * 128:(qt + 1) * 128])
                for pc in range(4):
                    w = STR.get()
                    for cc in range(2):
                        c = 2 * pc + cc
                        bi = pgen()
                        pp = PGEN[bi]
                        for k in range(8):
                            mm(pp, w.ap[:, k, cc * 128:(cc + 1) * 128], OM[k].ap[:, :N], k == 0, k == 7, [w, OM[k]],
                               out_ap=pp.ap[:, :N])
                        act(OO[c], pp.ap[:, :N], AF.Copy, [pp], out_ap=OO[c].ap[:, :N])
                        sq = tb()
                        act(sq, pp.ap[:, :N], AF.Square, [pp], out_ap=sq.ap[:, :N])
                        mm(PS_SS, ONES.ap, sq.ap[:, :N], c == 0, c == 7, [ONES, sq], out_ap=PS_SS.ap[:, :N], inc=True)
                postnorm(1, gi, PS_SS, OO, [o.ap[:, :N] for o in OO])

        for l in range(nlayers):
            last = l == nlayers - 1
            NLAM = layer_prologue(l)
            if "ffn0" in stages:
                ffn(l, 0, 0, [0, 1, 2, 3, 4])
            if "mix" in stages:
                mixer(l, last, NLAM)
            if "ffn1" in stages:
                ffn(l, 1, 2, [1, 2, 3, 4] if last else [0, 1, 2, 3, 4])
        for c in range(8):
            srcs = [RES[c][g] for g in range(1, 5)]
            B._deps("sp", srcs, [])
            nc.sync.dma_start(out=y_d[:, c, :], in_=arena[:, (o_res + (c * NT + NCTX) * 4) // 4:(o_res + (c + 1) * NT * 4) // 4]).then_inc(dsem_out["sem"], 16)
        nc.sync.wait_ge(dsem_out["sem"], 16 * 8)
        print("instr counts", B.cnt)
    return nc


def _pk(w, ncols):
    K, C = w.shape
    q = C // ncols
    return np.ascontiguousarray(w.reshape(8, 128, q, ncols).transpose(2, 1, 0, 3).reshape(q, 128, 8 * ncols))


def prep_shared(inp, nlayers=DEPTH):
    f = lambda a: np.asarray(a, dtype=np.float32)
    w_mod = f(inp["w_mod"]); b_mod = f(inp["b_mod"])
    L = DEPTH
    sh = {}
    sh["wmod"] = np.stack([_pk(w_mod[l], 512) for l in range(L)])
    sh["bmod"] = np.ascontiguousarray(b_mod.reshape(L, 72, 128).transpose(0, 2, 1))
    sh["npre"] = np.ascontiguousarray(f(inp["norm_pre"]).reshape(L, 24, 128).transpose(0, 2, 1))
    sh["npost"] = np.ascontiguousarray(f(inp["norm_post"]).reshape(L, 24, 128).transpose(0, 2, 1))
    wg = f(inp["ffn_w_gate"]); wu = f(inp["ffn_w_up"]); wd = f(inp["ffn_w_down"])
    sh["wg"] = np.stack([np.stack([_pk(wg[l, i], 256) for i in range(2)]) for l in range(L)])
    sh["wu"] = np.stack([np.stack([_pk(wu[l, i], 256) for i in range(2)]) for l in range(L)])
    sh["wd"] = np.ascontiguousarray(wd.reshape(L, 2, NJ, 128, 8, 128).transpose(0, 1, 4, 3, 2, 5).reshape(L, 2, 8, 128, NJ * 128))
    w_in = f(inp["w_in"])
    kv_cols = np.concatenate([np.arange(1280, 1792), np.arange(2304, 2432), np.arange(2432, 2560), np.arange(1792, 2304)])
    qc = np.array([1024 + kvh * 128 + g * 64 + d for g in range(2) for kvh in range(2) for d in range(64)])
    q_cols = np.concatenate([np.arange(512, 1024), qc, np.arange(0, 256), np.arange(256, 512)])
    sh["wkv"] = np.stack([_pk(w_in[l][:, kv_cols], 256) for l in range(L)])
    sh["wq"] = np.stack([_pk(w_in[l][:, q_cols], 256) for l in range(L)])
    sh["wout"] = np.stack([_pk(f(inp["w_out"])[l], 256) for l in range(L)])
    ws = f(inp["gmlp_w_s"])
    sh["wsT"] = np.ascontiguousarray(ws.transpose(0, 3, 1, 2).reshape(L, 128, 512))
    bs = f(inp["gmlp_b_s"])
    bsbc = np.zeros((L, 128, 2, 128), np.float32)
    for hp in range(2):
        for r in range(128):
            bsbc[:, r, hp, :] = bs[:, 2 * hp + r // 64, :]
    sh["bsbc"] = bsbc.reshape(L, 128, 256)
    sh["vgain"] = np.ascontiguousarray(np.broadcast_to(f(inp["gmlp_v_gain"]).reshape(L, 1, 256), (L, 128, 256)))
    sh["subln"] = np.ascontiguousarray(np.broadcast_to(f(inp["diff_subln"]).reshape(L, 1, 128), (L, 128, 128)))
    sh["dlam"] = np.ascontiguousarray(np.broadcast_to(f(inp["diff_lambda"]).reshape(L, 1, 256), (L, 128, 256)))
    sh["sink"] = np.ascontiguousarray(np.broadcast_to(f(inp["swa_sink"]).reshape(L, 1, 4), (L, 128, 4)))
    ident = np.eye(128, dtype=np.float32)
    pmm = np.zeros((128, 128), np.float32)
    for m in range(128):
        blk, d = m // 64, m % 64
        qd, r = d // 16, d % 16
        if qd % 2 == 0:
            pmm[blk * 64 + (qd + 1) * 16 + r, m] = -1.0
        else:
            pmm[blk * 64 + (qd - 1) * 16 + r, m] = 1.0
    ki = np.arange(128)[:, None]; qi = np.arange(128)[None, :]
    m0 = (qi <= ki).astype(np.float32)
    m2 = (ki <= qi).astype(np.float32)
    sh["cbf"] = np.ascontiguousarray(np.stack([ident, pmm, m0, m2], axis=1).reshape(128, 512))
    s = np.arange(NLAT)
    row = (s // 64).astype(np.float32); col = (s % 64).astype(np.float32)
    inv = (10000.0 ** (-np.arange(16, dtype=np.float32) / 16)).astype(np.float32)
    ar = row[:, None] * inv[None, :]; ac = col[:, None] * inv[None, :]
    ang = np.concatenate([ar, ar, ac, ac], axis=-1).astype(np.float32)
    cosT = np.cos(ang).astype(np.float32).T; sinT = np.sin(ang).astype(np.float32).T
    sh["cosT"] = np.ascontiguousarray(np.concatenate([cosT, cosT], axis=0))
    sh["sinT"] = np.ascontiguousarray(np.concatenate([sinT, sinT], axis=0))
    return sh


def prep_core(inp, b):
    f = lambda a: np.asarray(a, dtype=np.float32)
    tok = np.concatenate([f(inp["ctx"])[b], f(inp["x"])[b]], axis=0)
    xT = np.ascontiguousarray(tok.T.reshape(8, 128, NT).transpose(1, 0, 2))
    cc = np.stack([f(inp["c"])[b], f(inp["c_ctx"])], axis=-1)
    cT = np.ascontiguousarray(cc.reshape(8, 128, 2).transpose(1, 0, 2).reshape(128, 16))
    return {"xT": xT, "cT": cT}


_NC_CACHE = {}


def run(inp, nlayers=DEPTH, trace=False, stages=("ffn0", "mix", "ffn1", "p1", "p2q", "B", "C"), ncores=8):
    key = (nlayers, tuple(stages))
    if key not in _NC_CACHE:
        _NC_CACHE[key] = build(nlayers, stages)
    nc = _NC_CACHE[key]
    sh = prep_shared(inp, nlayers)
    in_maps = []
    for b in range(ncores):
        m = dict(sh)
        m.update(prep_core(inp, b))
        in_maps.append(m)
    res = run_bass_kernel_spmd(nc, in_maps, core_ids=list(range(ncores)))
    out = np.empty((ncores, NLAT, D), np.float32)
    for b in range(ncores):
        yT = res.results[b]["yT"]
        out[b] = yT.transpose(2, 1, 0).reshape(NLAT, D)
    return out


def kernel(**inputs):
    return run(inputs, DEPTH)
```

```python
import math
import numpy as np
import concourse.bass as bass
import concourse.mybir as mybir
from concourse.bass_utils import run_bass_kernel_spmd

F32 = mybir.dt.float32
BF16 = mybir.dt.bfloat16
AF = mybir.ActivationFunctionType
ALU = mybir.AluOpType
AX = mybir.AxisListType

D = 1024
DEPTH = 4
NCTX = 256
NLAT = 2048
NT = NCTX + NLAT
DFF = 2816
NJ = DFF // 128
EPS = 1e-6
GROUPS = [(0, 256, 1), (256, 512, 0), (768, 512, 0), (1280, 512, 0), (1792, 512, 0)]
GC1 = 0.7978845608028654
ARENA_BYTES = 212000


class Tile:
    __slots__ = ("ap", "space", "lo", "hi", "w", "r", "ov")

    def __init__(self, ap, space, lo, hi):
        self.ap = ap
        self.space = space
        self.lo = lo
        self.hi = hi
        self.w = None
        self.r = {}
        self.ov = None


class Builder:
    def __init__(self, nlayers):
        self.nl = nlayers
        self.nc = bass.Bass("TRN2", target_bir_lowering=False)
        self.tiles = []
        self.cnt = {}
        self.sem = {}
        self.eng = {}
        self.waited = {}
        self.dsems = []

    def reg_engine(self, name, eng, sem):
        self.eng[name] = eng
        self.sem[name] = sem
        self.cnt[name] = 0
        self.waited[name] = {}

    def mk(self, ap, space, lo, hi):
        t = Tile(ap, space, lo, hi)
        t.ov = [t]
        for o in self.tiles:
            if o.space == space and o.lo < hi and lo < o.hi:
                o.ov.append(t)
                t.ov.append(o)
        self.tiles.append(t)
        return t

    def _wait(self, en, key, semobj, val):
        w = self.waited[en]
        if w.get(key, 0) >= val:
            return
        self.eng[en].wait_ge(semobj, val)
        w[key] = val

    def _deps(self, en, R, W):
        need = {}

        def add(mark):
            if mark is None:
                return
            key = mark[0]
            if need.get(key, (None, 0))[1] < mark[2]:
                need[key] = (mark[1], mark[2])

        for t in R:
            for o in t.ov:
                add(o.w)
        for t in W:
            for o in t.ov:
                add(o.w)
                for mk_ in o.r.values():
                    add(mk_)
        for key, (semobj, val) in need.items():
            if key == "pe" and en == "pe":
                continue
            self._wait(en, key, semobj, val)

    def op(self, en, fn, R=(), W=(), inc=True):
        self._deps(en, R, W)
        ins = fn()
        c = self.cnt[en] + 1
        if inc:
            ins.then_inc(self.sem[en], 1)
            self.cnt[en] = c
        mark = (en, self.sem[en], c)
        for t in R:
            t.r[en] = mark
        for t in W:
            t.w = mark
            t.r = {}
        return ins

    def dma(self, q, out_t, in_ap, dsem, out_ap=None, R=()):
        self._deps(q, R, [out_t])
        ins = self.eng[q].dma_start(out=(out_ap if out_ap is not None else out_t.ap), in_=in_ap)
        dsem["val"] += 16
        ins.then_inc(dsem["sem"], 16)
        out_t.w = (dsem["key"], dsem["sem"], dsem["val"])
        out_t.r = {}


def build(nlayers=DEPTH, stages=("ffn0", "mix", "ffn1", "p1", "p2q", "B", "C")):
    B = Builder(nlayers)
    nc = B.nc
    dr = {}

    def din(name, shape):
        dr[name] = nc.dram_tensor(name, list(shape), F32, kind="ExternalInput").ap()
        return dr[name]

    xT_d = din("xT", [128, 8, NT])
    cT_d = din("cT", [128, 16])
    wmod_d = din("wmod", [DEPTH, 18, 128, 8 * 512])
    bmod_d = din("bmod", [DEPTH, 128, 72])
    npre_d = din("npre", [DEPTH, 128, 24])
    npost_d = din("npost", [DEPTH, 128, 24])
    wg_d = din("wg", [DEPTH, 2, 11, 128, 8 * 256])
    wu_d = din("wu", [DEPTH, 2, 11, 128, 8 * 256])
    wd_d = din("wd", [DEPTH, 2, 8, 128, NJ * 128])
    wkv_d = din("wkv", [DEPTH, 5, 128, 8 * 256])
    wq_d = din("wq", [DEPTH, 5, 128, 8 * 256])
    wout_d = din("wout", [DEPTH, 4, 128, 8 * 256])
    wsT_d = din("wsT", [DEPTH, 128, 512])
    bsbc_d = din("bsbc", [DEPTH, 128, 256])
    vgain_d = din("vgain", [DEPTH, 128, 256])
    subln_d = din("subln", [DEPTH, 128, 128])
    dlam_d = din("dlam", [DEPTH, 128, 256])
    sink_d = din("sink", [DEPTH, 128, 4])
    cbf_d = din("cbf", [128, 4 * 128])
    cos_d = din("cosT", [128, NLAT])
    sin_d = din("sinT", [128, NLAT])
    y_d = nc.dram_tensor("yT", [128, 8, NLAT], F32, kind="ExternalOutput").ap()

    import contextlib
    with contextlib.ExitStack() as es:
        arena = es.enter_context(nc.sbuf_tensor("arena", [128, ARENA_BYTES // 4], F32))
        psum = es.enter_context(nc.psum_tensor("psum", [128, 4096], F32))
        for nm, eng in (("pe", nc.tensor), ("act", nc.scalar), ("dve", nc.vector), ("pool", nc.gpsimd), ("sp", nc.sync)):
            B.reg_engine(nm, eng, es.enter_context(nc.semaphore("s_" + nm)))

        def newdsem(name):
            s = es.enter_context(nc.semaphore("d_" + name))
            return {"sem": s, "val": 0, "key": "d_" + name}

        off = [0]

        def alloc(nbytes):
            o = off[0]
            off[0] = (o + nbytes + 63) // 64 * 64
            assert off[0] <= ARENA_BYTES, off[0]
            return o

        def sb(o, dtype, n, shape=None):
            es_ = 4 if dtype == F32 else 2
            assert o % 4 == 0 and (n * es_) % 4 == 0, (o, n)
            ap = arena[:, o // 4:(o + n * es_) // 4]
            if dtype != F32:
                ap = ap.bitcast(dtype)
            if shape is not None:
                names = " ".join("a%d" % i for i in range(len(shape)))
                kw = {"a%d" % i: s for i, s in enumerate(shape)}
                ap = ap.rearrange("p (%s) -> p %s" % (names, names), **kw)
            return B.mk(ap, "S", o, o + n * es_)

        def ps(bank, n, dtype=F32, colo=0, shape=None):
            es_ = 4 if dtype == F32 else 2
            lo = bank * 2048 + colo * 4
            ap = psum[:, lo // 4:(lo + n * es_) // 4]
            if dtype != F32:
                ap = ap.bitcast(dtype)
            if shape is not None:
                names = " ".join("a%d" % i for i in range(len(shape)))
                kw = {"a%d" % i: s for i, s in enumerate(shape)}
                ap = ap.rearrange("p (%s) -> p %s" % (names, names), **kw)
            return B.mk(ap, "P", bank * 2048, bank * 2048 + 2048)

        o_res = alloc(8 * NT * 4)
        RES = [[sb(o_res + (c * NT + t0) * 4, F32, N) for (t0, N, _) in GROUPS] for c in range(8)]
        o_c = alloc(4 * 128 * 2)
        CBF = sb(o_c, BF16, 512, [4, 128])
        ident, pm, mask0, mask2 = (CBF.ap[:, i, :] for i in range(4))
        ONES = sb(alloc(256), BF16, 128)
        EPS_T = sb(alloc(64), F32, 4)
        CT = sb(alloc(64), F32, 16, [8, 2])
        ST = sb(alloc(32), BF16, 16, [8, 2])
        CTMP = sb(alloc(64), F32, 16, [8, 2])
        MODV = sb(alloc(576), F32, 144, [72, 2])
        BMOD = sb(alloc(288), F32, 72)
        NPRE = sb(alloc(96), F32, 24)
        NPOST = sb(alloc(96), F32, 24)
        GM = sb(alloc(192), F32, 48, [3, 2, 8])
        GG = sb(alloc(192), F32, 48, [3, 2, 8])
        WST = sb(alloc(1024), BF16, 512, [4, 128])
        WSTF = sb(alloc(2048), F32, 512)
        BSBC = sb(alloc(1024), F32, 256, [2, 128])
        VGAIN = sb(alloc(1024), F32, 256)
        SUBLN = sb(alloc(512), F32, 128)
        DLAM = sb(alloc(1024), F32, 256, [4, 64])
        SINK = sb(alloc(64), F32, 4)
        SMALL = sb(alloc(128), F32, 32)
        small_lo = SMALL.lo

        def small(i, n=1):
            return sb(small_lo + 4 * i, F32, n)

        TF = [sb(alloc(2048), F32, 512) for _ in range(6)]
        TB = [sb(alloc(1024), BF16, 512) for _ in range(4)]
        RSP = [sb(alloc(2048), F32, 512) for _ in range(2)]
        rsi = [0]

        def rsp():
            rsi[0] ^= 1
            return RSP[rsi[0]]
        LS = small(0, 2)
        NLAM_T = small(2)
        SSV = small(4, 4)
        RR = small(12, 2)
        SSB = small(16)
        DEN = small(20, 4)
        tfi = [0]
        tbi = [0]

        def tf():
            tfi[0] = (tfi[0] + 1) % len(TF)
            return TF[tfi[0]]

        def tb():
            tbi[0] = (tbi[0] + 1) % len(TB)
            return TB[tbi[0]]

        WS = [sb(alloc(4096), BF16, 2048, [8, 256]) for _ in range(4)]
        WS_sem = [newdsem("ws%d" % i) for i in range(4)]
        wsi = [0]

        def wslot():
            wsi[0] = (wsi[0] + 1) % 4
            return wsi[0]

        scratch0 = off[0]
        HMAX = 1024
        o_y = alloc(8 * HMAX * 2)
        o_h = alloc(NJ * HMAX * 2)
        o_wd = [alloc(NJ * 128 * 2) for _ in range(2)]
        WD = [sb(o, BF16, NJ * 128, [NJ, 128]) for o in o_wd]
        WD_sem = [newdsem("wd%d" % i) for i in range(2)]
        ffn_end = off[0]
        o_o_extra = alloc(8 * HMAX * 4 - 8 * HMAX * 2)
        o_o = None
        ffn_end = off[0]
        def o_off(c, loff):
            if c < 4:
                return o_y + (c * HMAX + loff) * 4
            return o_o_extra + ((c - 4) * HMAX + loff) * 4
        off[0] = scratch0
        o_ktb = alloc(4 * NT * 2)
        KTB = [[sb(o_ktb + (h * NT + kt * 128) * 2, BF16, 128) for kt in range(18)] for h in range(4)]
        o_ktc = alloc(NT * 2)
        KTC = [sb(o_ktc + kt * 128 * 2, BF16, 128) for kt in range(18)]
        o_vb = alloc(18 * 4 * 130 * 2)
        VB = [sb(o_vb + kt * 4 * 130 * 2, BF16, 4 * 130, [4, 130]) for kt in range(18)]
        VBALL = sb(o_vb, BF16, 18 * 4 * 130, [72, 130])
        o_vc = alloc(18 * 2 * 66 * 2)
        VC = [sb(o_vc + kt * 2 * 66 * 2, BF16, 2 * 66, [2, 66]) for kt in range(18)]
        VCALL = sb(o_vc, BF16, 18 * 2 * 66, [36, 66])
        TAB = sb(alloc(4096), F32, 1024, [2, 512])
        TAB_sem = newdsem("tab")
        o_ax = alloc(8 * 512 * 2)
        o_qb = alloc(4 * 512 * 2)
        o_qc = alloc(2 * 512 * 2)
        o_u = alloc(2 * 512 * 4)
        o_vp = alloc(4 * 512 * 2)
        mix_o_lo = o_ax
        assert off[0] - o_ax >= 8 * 512 * 4
        o_om = alloc(8 * 512 * 2)
        PT = [sb(alloc(1024), BF16, 512) for _ in range(3)]
        mix_end = off[0]
        pti = [0]

        def ptile():
            pti[0] = (pti[0] + 1) % 3
            return PT[pti[0]]

        off[0] = scratch0
        WM = [sb(alloc(8192), BF16, 4096, [8, 512]) for _ in range(2)]
        WM_sem = [newdsem("wm%d" % i) for i in range(2)]
        print("arena: persistent %d, ffn_end %d, mix_end %d" % (scratch0, ffn_end, mix_end))

        def psbank(b, n=512):
            return ps(b, n)
        PS_SS = psbank(0)
        PG = [psbank(0), psbank(1)]
        PU = [psbank(2), psbank(3)]
        PO = [psbank(4), psbank(5)]
        PSS2 = [psbank(6), psbank(7)]
        PGEN = [psbank(1), psbank(2)]
        PGEN_BF = [ps(1, 1024, BF16), ps(2, 1024, BF16)]
        PST = [psbank(3), psbank(4)]
        pgi = [0]
        psti = [0]

        def pgen():
            pgi[0] ^= 1
            return pgi[0]

        def pst():
            psti[0] ^= 1
            return PST[psti[0]]

        ACCB = []
        for a in range(8):
            ACCB.append(ps(5 + a // 3, 130, F32, (a % 3) * 130))
        ACCC = ps(5, 4 * 66, F32, 0, [4, 66])
        PMOD = ps(7, 144, F32, 0, [72, 2])

        dsem_misc = newdsem("misc")
        dsem_par = [newdsem("par0"), newdsem("par1")]
        dsem_res = newdsem("res")
        dsem_out = newdsem("out")

        def mm(out_t, lhsT, rhs, start, stop, R, inc=None, out_ap=None, skip=False):
            oap = out_ap if out_ap is not None else out_t.ap
            kw = {"skip_group_check": True} if skip else {}
            return B.op("pe", lambda: nc.tensor.matmul(oap, lhsT, rhs, start=start, stop=stop, **kw),
                        R=R, W=[out_t], inc=(stop if inc is None else inc))

        def act(out_t, in_ap, func, R, out_ap=None, **kw):
            oap = out_ap if out_ap is not None else out_t.ap
            return B.op("act", lambda: nc.scalar.activation(oap, in_ap, func, **kw), R=R, W=[out_t])

        def fixmarks(tiles_, ds):
            for t_ in tiles_:
                t_.w = (ds["key"], ds["sem"], ds["val"])

        def dve(fn, R, W):
            return B.op("dve", fn, R=R, W=W)

        for c in range(8):
            for gi, (t0, N, _) in enumerate(GROUPS):
                pass
        RES_ALL = [sb(o_res + c * NT * 4, F32, NT) for c in range(8)]
        for c in range(8):
            B.dma("sp", RES_ALL[c], xT_d[:, c, :], dsem_res)
        fixmarks(RES_ALL, dsem_res)
        dsem_misc2 = newdsem("misc2")
        B.dma("pool", CBF, cbf_d, dsem_misc2, out_ap=arena[:, o_c // 4:(o_c + 1024) // 4].bitcast(BF16))
        B.dma("sp", CT, cT_d, dsem_misc, out_ap=arena[:, CT.lo // 4:CT.lo // 4 + 16])
        B.op("pool", lambda: nc.gpsimd.memset(ONES.ap, 1.0), R=[], W=[ONES])
        B.op("pool", lambda: nc.gpsimd.memset(EPS_T.ap, EPS), R=[], W=[EPS_T])
        B.op("pool", lambda: nc.gpsimd.memset(EPS_T.ap[:, 1:2], 4 * EPS), R=[], W=[EPS_T])
        B.op("pool", lambda: nc.gpsimd.memset(EPS_T.ap[:, 2:3], 1.0), R=[], W=[EPS_T])
        act(CTMP, CT.ap, AF.Tanh, [CT], scale=0.5)
        dve(lambda: nc.vector.scalar_tensor_tensor(CTMP.ap, CTMP.ap, 1.0, CT.ap, ALU.add, ALU.mult), [CT, CTMP], [CTMP])
        B.op("act", lambda: nc.scalar.mul(ST.ap, CTMP.ap, 0.5), R=[CTMP], W=[ST])

        def load_layer_params(l):
            ds = dsem_par[l % 2]
            for t, d_ in ((BMOD, bmod_d[l]), (NPRE, npre_d[l]), (NPOST, npost_d[l]), (WSTF, wsT_d[l]),
                          (VGAIN, vgain_d[l]), (SUBLN, subln_d[l]), (SINK, sink_d[l])):
                B.dma("sp", t, d_, ds)
            B.dma("sp", BSBC, bsbc_d[l], ds, out_ap=arena[:, BSBC.lo // 4:BSBC.lo // 4 + 256])
            B.dma("sp", DLAM, dlam_d[l], ds, out_ap=arena[:, DLAM.lo // 4:DLAM.lo // 4 + 256])
            fixmarks([BMOD, NPRE, NPOST, WSTF, VGAIN, SUBLN, SINK, BSBC, DLAM], ds)

        def layer_prologue(l):
            lam_init = 0.8 - 0.6 * math.exp(-0.3 * l)
            load_layer_params(l)
            def ldm(s):
                B.dma("pool", WM[s % 2], wmod_d[l, s], WM_sem[s % 2],
                      out_ap=arena[:, WM[s % 2].lo // 4:WM[s % 2].lo // 4 + 2048].bitcast(BF16))
            ldm(0)
            for s in range(18):
                if s + 1 < 18:
                    ldm(s + 1)
                w = WM[s % 2]
                for mc in range(4):
                    ic = s * 4 + mc
                    for k in range(8):
                        mm(PMOD, w.ap[:, k, mc * 128:(mc + 1) * 128], ST.ap[:, k, :], k == 0, k == 7,
                           [w, ST], out_ap=PMOD.ap[:, ic, :])
            for j in range(2):
                dve(lambda j=j: nc.vector.tensor_tensor(MODV.ap[:, :, j], PMOD.ap[:, :, j], BMOD.ap, ALU.add),
                    [PMOD, BMOD], [MODV])
            for sub in range(3):
                wgt = 1.0 if sub == 1 else 0.5
                for j in range(2):
                    dve(lambda sub=sub, j=j: nc.vector.scalar_tensor_tensor(
                        GM.ap[:, sub, j, :], MODV.ap[:, (3 * sub + 1) * 8:(3 * sub + 1) * 8 + 8, j], 1.0,
                        NPRE.ap[:, sub * 8:sub * 8 + 8], ALU.add, ALU.mult), [MODV, NPRE], [GM])
                    dve(lambda sub=sub, j=j, wgt=wgt: nc.vector.scalar_tensor_tensor(
                        GG.ap[:, sub, j, :], MODV.ap[:, (3 * sub + 2) * 8:(3 * sub + 2) * 8 + 8, j], wgt,
                        NPOST.ap[:, sub * 8:sub * 8 + 8], ALU.mult, ALU.mult), [MODV, NPOST], [GG])
            act(WST, WSTF.ap.rearrange("p (h q) -> p h q", h=4), AF.Copy, [WSTF])
            B.op("act", lambda: nc.scalar.mul(SUBLN.ap, SUBLN.ap, 1.0 - lam_init), R=[SUBLN], W=[SUBLN])
            act(SINK, SINK.ap, AF.Exp, [SINK])
            lt = tf()
            dl4 = DLAM.ap.rearrange("p (a b) d -> p a b d", a=2)
            dve(lambda: nc.vector.tensor_tensor(lt.ap[:, 0:128].rearrange("p (a b) -> p a b", a=2),
                                                dl4[:, :, 0, :], dl4[:, :, 1, :], ALU.mult), [DLAM], [lt])
            dve(lambda: nc.vector.tensor_reduce(LS.ap, lt.ap[:, 0:128].rearrange("p (a b) -> p a b", a=2), AX.X, ALU.add),
                [lt], [LS])
            act(LS, LS.ap, AF.Exp, [LS])
            NLAM = NLAM_T
            dve(lambda: nc.vector.scalar_tensor_tensor(NLAM.ap, LS.ap[:, 1:2], -lam_init, LS.ap[:, 0:1], ALU.add, ALU.subtract),
                [LS], [NLAM])
            return NLAM

        def rstd_from(ps_t, N, scale, eps_ap):
            sd = rsp()
            act(sd, ps_t.ap[:, :N], AF.Ln, [ps_t, EPS_T], out_ap=sd.ap[:, :N], scale=scale, bias=eps_ap)
            act(sd, sd.ap[:, :N], AF.Exp, [sd], out_ap=sd.ap[:, :N], scale=-0.5)
            return sd

        def prenorm(sub, gi, outs, outaps):
            t0, N, j = GROUPS[gi]
            for c in range(8):
                sq = tb()
                act(sq, RES[c][gi].ap, AF.Square, [RES[c][gi]], out_ap=sq.ap[:, :N])
                mm(PS_SS, ONES.ap, sq.ap[:, :N], c == 0, c == 7, [ONES, sq], out_ap=PS_SS.ap[:, :N], inc=True)
            rs = rstd_from(PS_SS, N, 1.0 / D, EPS_T.ap[:, 0:1])
            for c in range(8):
                t = tf()
                dve(lambda c=c, t=t: nc.vector.tensor_tensor(t.ap[:, :N], RES[c][gi].ap, rs.ap[:, :N], ALU.mult),
                    [RES[c][gi], rs], [t])
                act(outs[c], t.ap[:, :N], AF.Identity, [t, GM, MODV], out_ap=outaps[c],
                    scale=GM.ap[:, sub, j, c:c + 1], bias=MODV.ap[:, (3 * sub) * 8 + c, j:j + 1])

        def postnorm(sub, gi, ps_ss, o_tiles, o_aps):
            t0, N, j = GROUPS[gi]
            rs = rstd_from(ps_ss, N, 1.0 / D, EPS_T.ap[:, 0:1])
            for c in range(8):
                t = tf()
                dve(lambda c=c, t=t: nc.vector.tensor_tensor(t.ap[:, :N], o_aps[c], rs.ap[:, :N], ALU.mult),
                    [o_tiles[c], rs], [t])
                dve(lambda c=c, t=t: nc.vector.scalar_tensor_tensor(
                    RES[c][gi].ap, t.ap[:, :N], GG.ap[:, sub, j, c:c + 1], RES[c][gi].ap, ALU.mult, ALU.add),
                    [t, GG, RES[c][gi]], [RES[c][gi]])

        def wload(slot, dram_ap):
            B.dma("pool", WS[slot], dram_ap, WS_sem[slot],
                  out_ap=arena[:, WS[slot].lo // 4:WS[slot].lo // 4 + 1024].bitcast(BF16))

        class Streamer:
            def __init__(self, aps):
                self.aps = aps
                self.issued = 0
                self.cur = 0
                self.slots = []

            def get(self):
                while self.issued < len(self.aps) and self.issued <= self.cur + 2:
                    s_ = wslot()
                    wload(s_, self.aps[self.issued])
                    self.slots.append(s_)
                    self.issued += 1
                s_ = self.slots[self.cur]
                self.cur += 1
                return WS[s_]

        def ffn(l, f, sub, gis):
            halves = [[0, 1], [2, 3], [4]] if 0 in gis else [[1, 2], [3, 4]]
            pending_first = [None]
            for ih, half in enumerate(halves):
                loff = {}
                o_ = 0
                for g in half:
                    loff[g] = o_
                    o_ += GROUPS[g][1]
                Y = {g: [sb(o_y + (c * HMAX + loff[g]) * 2, BF16, GROUPS[g][1]) for c in range(8)] for g in half}
                H = {g: [sb(o_h + (jj * HMAX + loff[g]) * 2, BF16, GROUPS[g][1]) for jj in range(NJ)] for g in half}
                O = {g: [sb(o_off(c, loff[g]), F32, GROUPS[g][1]) for c in range(8)] for g in half}
                for g in half:
                    prenorm(sub, g, Y[g], [y.ap for y in Y[g]])
                def ld1(q):
                    a, b_ = wslot(), wslot()
                    wload(a, wg_d[l, f, q])
                    wload(b_, wu_d[l, f, q])
                    return a, b_
                def ld2(c):
                    s = c % 2
                    B.dma("pool", WD[s], wd_d[l, f, c], WD_sem[s],
                          out_ap=arena[:, WD[s].lo // 4:WD[s].lo // 4 + NJ * 64].bitcast(BF16))
                nxt = pending_first[0] if pending_first[0] is not None else ld1(0)
                pending_first[0] = None
                for q in range(11):
                    if q == 9:
                        ld2(0)
                        ld2(1)
                    cur = nxt
                    if q + 1 < 11:
                        nxt = ld1(q + 1)
                    wgs, wus = WS[cur[0]], WS[cur[1]]
                    for g in half:
                        N = GROUPS[g][1]
                        for jj in range(2):
                            jh = 2 * q + jj
                            bi = pgen()
                            pg_, pu_ = PG[bi], PU[bi]
                            for k in range(8):
                                mm(pg_, wgs.ap[:, k, jj * 128:(jj + 1) * 128], Y[g][k].ap, k == 0, k == 7,
                                   [wgs, Y[g][k]], out_ap=pg_.ap[:, :N])
                            for k in range(8):
                                mm(pu_, wus.ap[:, k, jj * 128:(jj + 1) * 128], Y[g][k].ap, k == 0, k == 7,
                                   [wus, Y[g][k]], out_ap=pu_.ap[:, :N])
                            th = tf()
                            act(th, pg_.ap[:, :N], AF.Tanh, [pg_], out_ap=th.ap[:, :N], scale=0.5)
                            dve(lambda th=th, pg_=pg_, N=N: nc.vector.scalar_tensor_tensor(
                                th.ap[:, :N], th.ap[:, :N], 1.0, pg_.ap[:, :N], ALU.add, ALU.mult), [th, pg_], [th])
                            dve(lambda th=th, pu_=pu_, N=N, g=g, jh=jh: nc.vector.scalar_tensor_tensor(
                                H[g][jh].ap, th.ap[:, :N], 0.5, pu_.ap[:, :N], ALU.mult, ALU.mult), [th, pu_], [H[g][jh]])
                for c in range(8):
                    if c == 5 and ih + 1 < len(halves):
                        pending_first[0] = ld1(0)
                    wdt = WD[c % 2]
                    for ig, g in enumerate(half):
                        N = GROUPS[g][1]
                        po = PO[pgen()]
                        for jh in range(NJ):
                            mm(po, wdt.ap[:, jh, :], H[g][jh].ap, jh == 0, jh == NJ - 1, [wdt, H[g][jh]],
                               out_ap=po.ap[:, :N])
                        act(O[g][c], po.ap[:, :N], AF.Copy, [po])
                        sq = tb()
                        act(sq, po.ap[:, :N], AF.Square, [po], out_ap=sq.ap[:, :N])
                        mm(PSS2[ig], ONES.ap, sq.ap[:, :N], c == 0, c == 7, [ONES, sq], out_ap=PSS2[ig].ap[:, :N], inc=True)
                    if c + 2 < 8:
                        ld2(c + 2)
                for ig, g in enumerate(half):
                    postnorm(sub, g, PSS2[ig], O[g], [o.ap for o in O[g]])

        def rope(src_ps, N, out_t, out_ap, tabs):
            kb = tb()
            act(kb, src_ps.ap[:, :N], AF.Copy, [src_ps], out_ap=kb.ap[:, :N])
            bi = pgen()
            pr = PGEN[bi]
            mm(pr, pm, kb.ap[:, :N], True, True, [CBF, kb], out_ap=pr.ap[:, :N])
            t1 = tf()
            dve(lambda: nc.vector.tensor_tensor(t1.ap[:, :N], kb.ap[:, :N], TAB.ap[:, 0, :N], ALU.mult), [kb, TAB], [t1])
            t2 = tf()
            dve(lambda: nc.vector.tensor_tensor(t2.ap[:, :N], pr.ap[:, :N], TAB.ap[:, 1, :N], ALU.mult), [pr, TAB], [t2])
            dve(lambda: nc.vector.tensor_tensor(out_ap, t1.ap[:, :N], t2.ap[:, :N], ALU.add), [t1, t2], [out_t])

        def load_tab(gi):
            t0, N, _ = GROUPS[gi]
            s0 = t0 - NCTX
            B.dma("sp", TAB, cos_d[:, s0:s0 + 512], TAB_sem, out_ap=TAB.ap[:, 0, :])
            B.dma("sp", TAB, sin_d[:, s0:s0 + 512], TAB_sem, out_ap=TAB.ap[:, 1, :])

        def gelu2(src_ps_ap, src_t, n, out_t=None, out_ap=None):
            sq = tf()
            act(sq, src_ps_ap, AF.Square, [src_t], out_ap=sq.ap[:, :n])
            dve(lambda: nc.vector.scalar_tensor_tensor(sq.ap[:, :n], sq.ap[:, :n], 0.044715, src_ps_ap, ALU.mult, ALU.mult),
                [sq, src_t], [sq])
            dve(lambda: nc.vector.tensor_tensor(sq.ap[:, :n], sq.ap[:, :n], src_ps_ap, ALU.add), [sq, src_t], [sq])
            th = tf()
            act(th, sq.ap[:, :n], AF.Tanh, [sq], out_ap=th.ap[:, :n], scale=GC1)
            if out_t is None:
                out_t, out_ap = th, th.ap[:, :n]
            dve(lambda: nc.vector.scalar_tensor_tensor(out_ap, th.ap[:, :n], 1.0, src_ps_ap, ALU.add, ALU.mult),
                [th, src_t], [out_t])
            return out_t

        def mixer(l, last, NLAM):
            AXT = [sb(o_ax + c * 512 * 2, BF16, 512) for c in range(8)]
            for q4 in range(2):
                act(VBALL, ONES.ap[:, 0:72].rearrange("p (a b) -> p a b", b=2), AF.Copy, [ONES],
                    out_ap=VBALL.ap[:, q4 * 36:(q4 + 1) * 36, 128:130])
            act(VCALL, ONES.ap[:, 0:72].rearrange("p (a b) -> p a b", b=2), AF.Copy, [ONES], out_ap=VCALL.ap[:, :, 64:66])
            VPALL0 = sb(o_vp, BF16, 2048)
            B.op("act", lambda: nc.scalar.memzero(VPALL0.ap), R=[], W=[VPALL0])
            plist = []
            if "p1" in stages:
                for gi_ in range(5):
                    plist += [wkv_d[l, pc_] for pc_ in range(5)]
            for gi_ in range(5):
                if last and gi_ == 0:
                    continue
                if "p2q" in stages:
                    plist += [wq_d[l, pc_] for pc_ in range(5)]
                plist += [wout_d[l, pc_] for pc_ in range(4)]
            STR = Streamer(plist)
            for gi in (range(5) if "p1" in stages else []):
                t0, N, j = GROUPS[gi]
                lat = gi > 0
                if lat:
                    load_tab(gi)
                prenorm(1, gi, AXT, [a.ap[:, :N] for a in AXT])
                nt_ = N // 128
                for pc in range(5):
                    w = STR.get()
                    if pc < 2 or pc == 2:
                        chunks = [(0, 2 * pc), (1, 2 * pc + 1)] if pc < 2 else [(0, 4)]
                        for cc, ch in chunks:
                            bi = pgen()
                            pp = PGEN[bi]
                            for k in range(8):
                                mm(pp, w.ap[:, k, cc * 128:(cc + 1) * 128], AXT[k].ap[:, :N], k == 0, k == 7,
                                   [w, AXT[k]], out_ap=pp.ap[:, :N])
                            kt0 = t0 // 128
                            if ch < 4:
                                dst = [KTB[ch][kt0 + i] for i in range(nt_)]
                                dap = arena[:, (o_ktb + (ch * NT + t0) * 2) // 4:(o_ktb + (ch * NT + t0 + N) * 2) // 4].bitcast(BF16)
                            else:
                                dst = [KTC[kt0 + i] for i in range(nt_)]
                                dap = arena[:, (o_ktc + t0 * 2) // 4:(o_ktc + (t0 + N) * 2) // 4].bitcast(BF16)
                            big = B.mk(dap, "S", dst[0].lo, dst[-1].hi)
                            if lat:
                                rope(pp, N, big, dap, None)
                            else:
                                act(big, pp.ap[:, :N], AF.Copy, [pp])
                    if pc >= 2:
                        for tt in range(nt_):
                            kt = t0 // 128 + tt
                            bi = pgen()
                            pv = PGEN[bi]
                            c0 = 128 if pc == 2 else 0
                            ncol = 128 if pc == 2 else 256
                            for k in range(8):
                                mm(pv, AXT[k].ap[:, tt * 128:(tt + 1) * 128], w.ap[:, k, c0:c0 + ncol], k == 0, k == 7,
                                   [w, AXT[k]], out_ap=pv.ap[:, :ncol])
                            if pc == 2:
                                act(VC[kt], pv.ap[:, 0:128].rearrange("p (a b) -> p a b", a=2), AF.Copy, [pv],
                                    out_ap=VC[kt].ap[:, :, 0:64])
                            else:
                                h0 = 2 * (pc - 3)
                                act(VB[kt], pv.ap[:, 0:256].rearrange("p (a b) -> p a b", a=2), AF.Copy, [pv],
                                    out_ap=VB[kt].ap[:, h0:h0 + 2, 0:128])
            QB = [sb(o_qb + h * 512 * 2, BF16, 512) for h in range(4)]
            QC = [sb(o_qc + g * 512 * 2, BF16, 512) for g in range(2)]
            UT = [sb(o_u + hp * 512 * 4, F32, 512) for hp in range(2)]
            VP = [sb(o_vp + tt * 512 * 2, BF16, 512, [4, 128]) for tt in range(4)]
            VPALL = sb(o_vp, BF16, 2048)
            OM = [sb(o_om + c * 512 * 2, BF16, 512) for c in range(8)]
            OO = [sb(mix_o_lo + c * 512 * 4, F32, 512) for c in range(8)]
            for gi in range(5):
                if last and gi == 0:
                    continue
                t0, N, j = GROUPS[gi]
                lat = gi > 0
                nt_ = N // 128
                kt0 = t0 // 128
                if lat:
                    load_tab(gi)
                prenorm(1, gi, AXT, [a.ap[:, :N] for a in AXT])
                for pc in (range(5) if "p2q" in stages else []):
                    w = STR.get()
                    if pc < 3:
                        for cc in range(2):
                            bi = pgen()
                            pp = PGEN[bi]
                            for k in range(8):
                                mm(pp, w.ap[:, k, cc * 128:(cc + 1) * 128], AXT[k].ap[:, :N], k == 0, k == 7,
                                   [w, AXT[k]], out_ap=pp.ap[:, :N])
                            dst = QB[2 * pc + cc] if pc < 2 else QC[cc]
                            if lat:
                                rope(pp, N, dst, dst.ap[:, :N], None)
                            else:
                                act(dst, pp.ap[:, :N], AF.Copy, [pp], out_ap=dst.ap[:, :N])
                    elif pc == 3:
                        for cc in range(2):
                            bi = pgen()
                            pp = PGEN[bi]
                            for k in range(8):
                                mm(pp, w.ap[:, k, cc * 128:(cc + 1) * 128], AXT[k].ap[:, :N], k == 0, k == 7,
                                   [w, AXT[k]], out_ap=pp.ap[:, :N])
                            gelu2(pp.ap[:, :N], pp, N, out_t=UT[cc], out_ap=UT[cc].ap[:, :N])
                    else:
                        for tt in range(nt_):
                            bi = pgen()
                            pv = PGEN[bi]
                            for k in range(8):
                                mm(pv, AXT[k].ap[:, tt * 128:(tt + 1) * 128], w.ap[:, k, :], k == 0, k == 7,
                                   [w, AXT[k]], out_ap=pv.ap[:, :256])
                            g2 = gelu2(pv.ap[:, :256], pv, 256)
                            sqv = tf()
                            dve(lambda g2=g2, sqv=sqv: nc.vector.tensor_tensor(sqv.ap[:, :256], g2.ap[:, :256], g2.ap[:, :256], ALU.mult),
                                [g2], [sqv])
                            ssv = SSV
                            dve(lambda sqv=sqv, ssv=ssv: nc.vector.tensor_reduce(
                                ssv.ap, sqv.ap[:, :256].rearrange("p (h c) -> p h c", h=4), AX.X, ALU.add), [sqv], [ssv])
                            act(ssv, ssv.ap, AF.Ln, [ssv, EPS_T], scale=1.0 / 64, bias=EPS_T.ap[:, 1:2])
                            act(ssv, ssv.ap, AF.Exp, [ssv], scale=-0.5)
                            for hh in range(4):
                                dve(lambda g2=g2, ssv=ssv, hh=hh, tt=tt: nc.vector.scalar_tensor_tensor(
                                    VP[tt].ap[:, hh, (hh % 2) * 64:(hh % 2) * 64 + 64], g2.ap[:, hh * 64:(hh + 1) * 64],
                                    ssv.ap[:, hh:hh + 1], VGAIN.ap[:, hh * 64:(hh + 1) * 64], ALU.mult, ALU.mult),
                                    [g2, ssv, VGAIN], [VP[tt]])
                            for hp in range(2):
                                bi2 = pgen()
                                pm_ = PGEN[bi2]
                                mm(pm_, VP[tt].ap[:, 2 * hp, :], WST.ap[:, 2 * hp, :], True, False, [VP[tt], WST],
                                   out_ap=pm_.ap[:, :128])
                                mm(pm_, VP[tt].ap[:, 2 * hp + 1, :], WST.ap[:, 2 * hp + 1, :], False, True, [VP[tt], WST],
                                   out_ap=pm_.ap[:, :128])
                                t = tf()
                                dve(lambda t=t, pm_=pm_, hp=hp: nc.vector.tensor_tensor(t.ap[:, :128], pm_.ap[:, :128], BSBC.ap[:, hp, :], ALU.add),
                                    [pm_, BSBC], [t])
                                dve(lambda t=t, hp=hp, tt=tt: nc.vector.scalar_tensor_tensor(
                                    OM[hp].ap[:, tt * 128:(tt + 1) * 128], UT[hp].ap[:, tt * 128:(tt + 1) * 128], 0.5, t.ap[:, :128],
                                    ALU.mult, ALU.mult), [t, UT[hp]], [OM[hp]])
                kts = list(range(18)) if lat else [0, 1]
                for h in (range(4) if "B" in stages else []):
                    first_in_bank = set()
                    stepsB = [(ik, kt, m) for ik, kt in enumerate(kts) for m in range(2)]

                    def qkB(ik, kt, m):
                        st_ = pst()
                        mm(st_, KTB[h][kt].ap[m * 64:(m + 1) * 64, :], QB[h].ap[m * 64:(m + 1) * 64, :N], True, True,
                           [KTB[h][kt], QB[h]], out_ap=st_.ap[:, :N])
                        p_ = ptile()
                        act(p_, st_.ap[:, :N], AF.Exp, [st_], out_ap=p_.ap[:, :N], scale=0.125)
                        return p_

                    def pvB(ik, kt, m, p_):
                        for qt in range(nt_):
                            a = qt * 2 + m
                            bank = a // 3
                            st_flag = (ik == 0) and (bank not in first_in_bank)
                            first_in_bank.add(bank)
                            mm(ACCB[a], p_.ap[:, qt * 128:(qt + 1) * 128], VB[kt].ap[:, h, 0:130], st_flag,
                               ik == len(kts) - 1, [p_, VB[kt]], skip=True, inc=(ik == len(kts) - 1))

                    prevB = None
                    for stp in stepsB:
                        p_ = qkB(*stp)
                        if prevB is not None:
                            pvB(*prevB)
                        prevB = stp + (p_,)
                    pvB(*prevB)
                    for qt in range(nt_):
                        a1, a2 = ACCB[qt * 2], ACCB[qt * 2 + 1]
                        rr = RR
                        dve(lambda a1=a1, rr=rr: nc.vector.reciprocal(rr.ap[:, 0:1], a1.ap[:, 128:129]), [a1], [rr])
                        dve(lambda a2=a2, rr=rr: nc.vector.reciprocal(rr.ap[:, 1:2], a2.ap[:, 128:129]), [a2, rr], [rr])
                        dve(lambda rr=rr: nc.vector.tensor_tensor(rr.ap[:, 1:2], rr.ap[:, 1:2], NLAM.ap, ALU.mult), [rr, NLAM], [rr])
                        o1 = tf()
                        dve(lambda a1=a1, rr=rr, o1=o1: nc.vector.tensor_scalar(o1.ap[:, :128], a1.ap[:, 0:128], rr.ap[:, 0:1], None, ALU.mult),
                            [a1, rr], [o1])
                        dve(lambda a2=a2, rr=rr, o1=o1: nc.vector.scalar_tensor_tensor(
                            o1.ap[:, :128], a2.ap[:, 0:128], rr.ap[:, 1:2], o1.ap[:, :128], ALU.mult, ALU.add), [a2, rr, o1], [o1])
                        sqt = tf()
                        ssb = SSB
                        act(sqt, o1.ap[:, :128], AF.Square, [o1], out_ap=sqt.ap[:, :128])
                        dve(lambda sqt=sqt, ssb=ssb: nc.vector.tensor_reduce(ssb.ap, sqt.ap[:, :128], AX.X, ALU.add), [sqt], [ssb])
                        act(ssb, ssb.ap, AF.Ln, [ssb, EPS_T], scale=1.0 / 128, bias=EPS_T.ap[:, 0:1])
                        act(ssb, ssb.ap, AF.Exp, [ssb], scale=-0.5)
                        ob = tb()
                        dve(lambda o1=o1, ssb=ssb, ob=ob: nc.vector.scalar_tensor_tensor(
                            ob.ap[:, :128], o1.ap[:, :128], ssb.ap, SUBLN.ap, ALU.mult, ALU.mult), [o1, ssb, SUBLN], [ob])
                        bi = pgen()
                        ptp = PGEN[bi]
                        mm(ptp, ob.ap[:, :128], ident, True, True, [ob, CBF], out_ap=ptp.ap[:, :128])
                        act(OM[2 + h], ptp.ap[:, :128], AF.Copy, [ptp], out_ap=OM[2 + h].ap[:, qt * 128:(qt + 1) * 128])
                for qt in (range(nt_) if "C" in stages else []):
                    if lat:
                        n = (t0 - NCTX) // 128 + qt
                        kl = [(0, None), (1, None)]
                        if n - 1 >= 0:
                            kl.append((n - 1 + 2, mask0))
                        kl.append((n + 2, None))
                        if n + 1 <= 15:
                            kl.append((n + 1 + 2, mask2))
                    else:
                        kl = [(0, None), (1, None)]
                    def qkC(ik, kt, msk):
                        for hd in range(4):
                            kvh, g_ = hd // 2, hd % 2
                            st_ = PST[kvh]
                            mm(st_, KTC[kt].ap[kvh * 64:(kvh + 1) * 64, :], QC[g_].ap[kvh * 64:(kvh + 1) * 64, qt * 128:(qt + 1) * 128],
                               True, True, [KTC[kt], QC[g_]], out_ap=st_.ap[:, g_ * 128:(g_ + 1) * 128], skip=True, inc=True)
                        p_ = ptile()
                        for kvh in range(2):
                            act(p_, PST[kvh].ap[:, 0:256], AF.Exp, [PST[kvh]], out_ap=p_.ap[:, kvh * 256:(kvh + 1) * 256], scale=0.125)
                        if msk is not None:
                            for hh in range(4):
                                dve(lambda p_=p_, msk=msk, hh=hh: nc.vector.tensor_tensor(
                                    p_.ap[:, hh * 128:(hh + 1) * 128], p_.ap[:, hh * 128:(hh + 1) * 128], msk, ALU.mult), [p_, CBF], [p_])
                        return p_

                    def pvC(ik, kt, msk, p_):
                        for hd in range(4):
                            kvh = hd // 2
                            mm(ACCC, p_.ap[:, hd * 128:(hd + 1) * 128], VC[kt].ap[:, kvh, 0:66], (ik == 0 and hd == 0),
                               ik == len(kl) - 1, [p_, VC[kt]], out_ap=ACCC.ap[:, hd, :], skip=True,
                               inc=(ik == len(kl) - 1 and hd == 3))

                    prevC = None
                    for ik, (kt, msk) in enumerate(kl):
                        p_ = qkC(ik, kt, msk)
                        if prevC is not None:
                            pvC(*prevC)
                        prevC = (ik, kt, msk, p_)
                    pvC(*prevC)
                    den = DEN
                    dve(lambda den=den: nc.vector.tensor_tensor(den.ap, ACCC.ap[:, :, 64], SINK.ap, ALU.add), [ACCC, SINK], [den])
                    act(den, den.ap, AF.Ln, [den])
                    act(den, den.ap, AF.Exp, [den], scale=-1.0)
                    oc = tb()
                    for hh in range(4):
                        dve(lambda den=den, oc=oc, hh=hh: nc.vector.tensor_scalar(
                            oc.ap[:, hh * 64:(hh + 1) * 64], ACCC.ap[:, hh, 0:64], den.ap[:, hh:hh + 1], None, ALU.mult),
                            [ACCC, den], [oc])
                    for i in range(2):
                        bi = pgen()
                        ptp = PGEN[bi]
                        mm(ptp, oc.ap[:, i * 128:(i + 1) * 128], ident, True, True, [oc, CBF], out_ap=ptp.ap[:, :128])
                        act(OM[6 + i], ptp.ap[:, :128], AF.Copy, [ptp], out_ap=OM[6 + i].ap[:, qt * 128:(qt + 1) * 128])
                for pc in range(4):
                    w = STR.get()
                    for cc in range(2):
                        c = 2 * pc + cc
                        bi = pgen()
                        pp = PGEN[bi]
                        for k in range(8):
                            mm(pp, w.ap[:, k, cc * 128:(cc + 1) * 128], OM[k].ap[:, :N], k == 0, k == 7, [w, OM[k]],
                               out_ap=pp.ap[:, :N])
                        act(OO[c], pp.ap[:, :N], AF.Copy, [pp], out_ap=OO[c].ap[:, :N])
                        sq = tb()
                        act(sq, pp.ap[:, :N], AF.Square, [pp], out_ap=sq.ap[:, :N])
                        mm(PS_SS, ONES.ap, sq.ap[:, :N], c == 0, c == 7, [ONES, sq], out_ap=PS_SS.ap[:, :N], inc=True)
                postnorm(1, gi, PS_SS, OO, [o.ap[:, :N] for o in OO])

        for l in range(nlayers):
            last = l == nlayers - 1
            NLAM = layer_prologue(l)
            if "ffn0" in stages:
                ffn(l, 0, 0, [0, 1, 2, 3, 4])
            if "mix" in stages:
                mixer(l, last, NLAM)
            if "ffn1" in stages:
                ffn(l, 1, 2, [1, 2, 3, 4] if last else [0, 1, 2, 3, 4])
        for c in range(8):
            srcs = [RES[c][g] for g in range(1, 5)]
            B._deps("sp", srcs, [])
            nc.sync.dma_start(out=y_d[:, c, :], in_=arena[:, (o_res + (c * NT + NCTX) * 4) // 4:(o_res + (c + 1) * NT * 4) // 4]).then_inc(dsem_out["sem"], 16)
        nc.sync.wait_ge(dsem_out["sem"], 16 * 8)
        print("instr counts", B.cnt)
    return nc


def _pk(w, ncols):
    K, C = w.shape
    q = C // ncols
    return np.ascontiguousarray(w.reshape(8, 128, q, ncols).transpose(2, 1, 0, 3).reshape(q, 128, 8 * ncols))


def prep_shared(inp, nlayers=DEPTH):
    f = lambda a: np.asarray(a, dtype=np.float32)
    w_mod = f(inp["w_mod"]); b_mod = f(inp["b_mod"])
    L = DEPTH
    sh = {}
    sh["wmod"] = np.stack([_pk(w_mod[l], 512) for l in range(L)])
    sh["bmod"] = np.ascontiguousarray(b_mod.reshape(L, 72, 128).transpose(0, 2, 1))
    sh["npre"] = np.ascontiguousarray(f(inp["norm_pre"]).reshape(L, 24, 128).transpose(0, 2, 1))
    sh["npost"] = np.ascontiguousarray(f(inp["norm_post"]).reshape(L, 24, 128).transpose(0, 2, 1))
    wg = f(inp["ffn_w_gate"]); wu = f(inp["ffn_w_up"]); wd = f(inp["ffn_w_down"])
    sh["wg"] = np.stack([np.stack([_pk(wg[l, i], 256) for i in range(2)]) for l in range(L)])
    sh["wu"] = np.stack([np.stack([_pk(wu[l, i], 256) for i in range(2)]) for l in range(L)])
    sh["wd"] = np.ascontiguousarray(wd.reshape(L, 2, NJ, 128, 8, 128).transpose(0, 1, 4, 3, 2, 5).reshape(L, 2, 8, 128, NJ * 128))
    w_in = f(inp["w_in"])
    kv_cols = np.concatenate([np.arange(1280, 1792), np.arange(2304, 2432), np.arange(2432, 2560), np.arange(1792, 2304)])
    qc = np.array([1024 + kvh * 128 + g * 64 + d for g in range(2) for kvh in range(2) for d in range(64)])
    q_cols = np.concatenate([np.arange(512, 1024), qc, np.arange(0, 256), np.arange(256, 512)])
    sh["wkv"] = np.stack([_pk(w_in[l][:, kv_cols], 256) for l in range(L)])
    sh["wq"] = np.stack([_pk(w_in[l][:, q_cols], 256) for l in range(L)])
    sh["wout"] = np.stack([_pk(f(inp["w_out"])[l], 256) for l in range(L)])
    ws = f(inp["gmlp_w_s"])
    sh["wsT"] = np.ascontiguousarray(ws.transpose(0, 3, 1, 2).reshape(L, 128, 512))
    bs = f(inp["gmlp_b_s"])
    bsbc = np.zeros((L, 128, 2, 128), np.float32)
    for hp in range(2):
        for r in range(128):
            bsbc[:, r, hp, :] = bs[:, 2 * hp + r // 64, :]
    sh["bsbc"] = bsbc.reshape(L, 128, 256)
    sh["vgain"] = np.ascontiguousarray(np.broadcast_to(f(inp["gmlp_v_gain"]).reshape(L, 1, 256), (L, 128, 256)))
    sh["subln"] = np.ascontiguousarray(np.broadcast_to(f(inp["diff_subln"]).reshape(L, 1, 128), (L, 128, 128)))
    sh["dlam"] = np.ascontiguousarray(np.broadcast_to(f(inp["diff_lambda"]).reshape(L, 1, 256), (L, 128, 256)))
    sh["sink"] = np.ascontiguousarray(np.broadcast_to(f(inp["swa_sink"]).reshape(L, 1, 4), (L, 128, 4)))
    ident = np.eye(128, dtype=np.float32)
    pmm = np.zeros((128, 128), np.float32)
    for m in range(128):
        blk, d = m // 64, m % 64
        qd, r = d // 16, d % 16
        if qd % 2 == 0:
            pmm[blk * 64 + (qd + 1) * 16 + r, m] = -1.0
        else:
            pmm[blk * 64 + (qd - 1) * 16 + r, m] = 1.0
    ki = np.arange(128)[:, None]; qi = np.arange(128)[None, :]
    m0 = (qi <= ki).astype(np.float32)
    m2 = (ki <= qi).astype(np.float32)
    sh["cbf"] = np.ascontiguousarray(np.stack([ident, pmm, m0, m2], axis=1).reshape(128, 512))
    s = np.arange(NLAT)
    row = (s // 64).astype(np.float32); col = (s % 64).astype(np.float32)
    inv = (10000.0 ** (-np.arange(16, dtype=np.float32) / 16)).astype(np.float32)
    ar = row[:, None] * inv[None, :]; ac = col[:, None] * inv[None, :]
    ang = np.concatenate([ar, ar, ac, ac], axis=-1).astype(np.float32)
    cosT = np.cos(ang).astype(np.float32).T; sinT = np.sin(ang).astype(np.float32).T
    sh["cosT"] = np.ascontiguousarray(np.concatenate([cosT, cosT], axis=0))
    sh["sinT"] = np.ascontiguousarray(np.concatenate([sinT, sinT], axis=0))
    return sh


def prep_core(inp, b):
    f = lambda a: np.asarray(a, dtype=np.float32)
    tok = np.concatenate([f(inp["ctx"])[b], f(inp["x"])[b]], axis=0)
    xT = np.ascontiguousarray(tok.T.reshape(8, 128, NT).transpose(1, 0, 2))
    cc = np.stack([f(inp["c"])[b], f(inp["c_ctx"])], axis=-1)
    cT = np.ascontiguousarray(cc.reshape(8, 128, 2).transpose(1, 0, 2).reshape(128, 16))
    return {"xT": xT, "cT": cT}


_NC_CACHE = {}


def run(inp, nlayers=DEPTH, trace=False, stages=("ffn0", "mix", "ffn1", "p1", "p2q", "B", "C"), ncores=8):
    key = (nlayers, tuple(stages))
    if key not in _NC_CACHE:
        _NC_CACHE[key] = build(nlayers, stages)
    nc = _NC_CACHE[key]
    sh = prep_shared(inp, nlayers)
    in_maps = []
    for b in range(ncores):
        m = dict(sh)
        m.update(prep_core(inp, b))
        in_maps.append(m)
    res = run_bass_kernel_spmd(nc, in_maps, core_ids=list(range(ncores)))
    out = np.empty((ncores, NLAT, D), np.float32)
    for b in range(ncores):
        yT = res.results[b]["yT"]
        out[b] = yT.transpose(2, 1, 0).reshape(NLAT, D)
    return out


def kernel(**inputs):
    return run(inputs, DEPTH)
```
